# Optimizing a Trainium2 kernel written in Bass

```python
import numpy as np
import jax, jax.numpy as jnp
from jax import lax

D_MODEL = 1024
BATCH = 8
SEQ = 2048
DEPTH = 4
DEC_BATCH = 2
DEC_SEQ = 16384
PAST_LEN = 128

D_MIX = D_MODEL
N_MIXERS = 4
D_BRANCH = D_MIX // N_MIXERS
N_GROUPS = 4
D_GROUP = D_BRANCH // N_GROUPS
POOL_WINDOWS = (2, 4, 8, 16)
CHUNK = 128
GRID_W = 64
NA_KH_MAX = 8
NA_KW = 16
N_IN_SLICES = 11
D_IN = N_IN_SLICES * D_BRANCH
RMS_EPS = 1e-6
LN_EPS = 1e-5

kernel_name = "hybrid_parallel_group_encoder"


def rmsnorm(x, g):
    xf = x.astype(jnp.float32)
    y = xf * lax.rsqrt(jnp.mean(xf * xf, axis=-1, keepdims=True) + RMS_EPS)
    return (y * g.astype(jnp.float32)).astype(x.dtype)


def pool_mixer(a, w_pool, pool_scale):
    B, L, _ = a.shape
    ag = a.astype(jnp.float32).reshape(B, L, N_GROUPS, D_GROUP)
    cs = jnp.concatenate([jnp.zeros((B, 1, N_GROUPS, D_GROUP), jnp.float32),
                          lax.cumsum(ag, axis=1)], axis=1)
    t = np.arange(L)
    outs = []
    for g, w in enumerate(POOL_WINDOWS):
        lo = np.clip(t - w // 2, 0, L)
        hi = np.clip(t - w // 2 + w, 0, L)
        cnt = (hi - lo).astype(np.float32)[None, :, None]
        cs_g = cs[:, :, g]
        mean = (jnp.take(cs_g, hi, axis=1) - jnp.take(cs_g, lo, axis=1)) / cnt
        outs.append(mean - ag[:, :, g])
    p = jnp.stack(outs, axis=2)
    y = jnp.einsum('blgc,gcd->blgd', p, w_pool.astype(jnp.float32))
    y = y.reshape(B, L, D_BRANCH) * pool_scale.astype(jnp.float32)
    return y.astype(a.dtype)


def sgu_mixer(u, v, sgu_norm_g, sgu_w, sgu_b):
    B, L, _ = u.shape
    u = jax.nn.gelu(u)
    vf = jax.nn.gelu(v).astype(jnp.float32)
    mu = jnp.mean(vf, axis=-1, keepdims=True)
    var = jnp.mean((vf - mu) ** 2, axis=-1, keepdims=True)
    vn = (vf - mu) * lax.rsqrt(var + LN_EPS) * sgu_norm_g.astype(jnp.float32)
    vc = vn.reshape(B, L // CHUNK, CHUNK, N_GROUPS, D_GROUP)
    s = jnp.einsum('hpq,bnqhc->bnphc', sgu_w.astype(jnp.float32), vc)
    s = s + sgu_b.astype(jnp.float32).T[:, :, None]
    return (u.astype(jnp.float32) * s.reshape(B, L, D_BRANCH)).astype(u.dtype)


def fourier_mixer(f, fnet_w):
    B, L, _ = f.shape
    ff = f.astype(jnp.float32).reshape(B, L, N_GROUPS, D_GROUP)
    z = jnp.fft.fft2(ff, axes=(1, 3), norm='ortho').real
    y = jnp.einsum('blgc,gcd->blgd', z, fnet_w.astype(jnp.float32))
    return y.reshape(B, L, D_BRANCH).astype(f.dtype)


def na_mixer(q, k, v, na_rpb):
    B, L, _ = q.shape
    rows = L // GRID_W
    kh = min(NA_KH_MAX, rows)
    scale = D_GROUP ** -0.5
    qg = q.reshape(B, rows, GRID_W, N_GROUPS, D_GROUP)
    kg = k.reshape(B, rows, GRID_W, N_GROUPS, D_GROUP)
    vg = v.reshape(B, rows, GRID_W, N_GROUPS, D_GROUP)
    r = np.arange(rows)
    rs = np.clip(r - kh // 2, 0, rows - kh)
    c = np.arange(GRID_W)
    cst = np.clip(c - NA_KW // 2, 0, GRID_W - NA_KW)
    col_idx = cst[:, None] + np.arange(NA_KW)[None, :]
    dc = col_idx - c[:, None]
    rpb_c = na_rpb[:, :, dc + NA_KW - 1]

    def row_block(args):
        q_row, r_i, rs_i = args
        k_rows = lax.dynamic_slice_in_dim(kg, rs_i, kh, axis=1)
        v_rows = lax.dynamic_slice_in_dim(vg, rs_i, kh, axis=1)
        k_sel = k_rows[:, :, col_idx]
        v_sel = v_rows[:, :, col_idx]
        dr = rs_i + jnp.arange(kh) - r_i
        bias = rpb_c[:, dr + NA_KH_MAX - 1]
        bias = jnp.transpose(bias, (0, 2, 1, 3))[None].astype(jnp.float32)
        s = jnp.einsum('bqhd,bkqwhd->bhqkw', q_row, k_sel,
                       preferred_element_type=jnp.float32) * scale + bias
        p = jax.nn.softmax(s.reshape(B, N_GROUPS, GRID_W, kh * NA_KW), axis=-1)
        p = p.reshape(B, N_GROUPS, GRID_W, kh, NA_KW).astype(v.dtype)
        return jnp.einsum('bhqkw,bkqwhd->bqhd', p, v_sel)

    out = lax.map(row_block, (jnp.moveaxis(qg, 1, 0),
                              jnp.asarray(r, jnp.int32), jnp.asarray(rs, jnp.int32)))
    return jnp.moveaxis(out, 0, 1).reshape(B, L, D_BRANCH)


def mixer_layer(x, c, norm_g, w_ada, b_ada, w_in, w_out, pool_w, pool_scale,
                sgu_norm_g, sgu_w, sgu_b, fnet_w, na_rpb):
    mod = jax.nn.silu(c) @ w_ada + b_ada
    shift, scl, gate = jnp.split(mod, 3, axis=-1)
    h = rmsnorm(x, norm_g) * (1.0 + scl[:, None]) + shift[:, None]
    z = h @ w_in
    (a_in, a_gate, b_u, b_v, b_gate, c_in, c_gate,
     d_q, d_k, d_v, d_gate) = jnp.split(z, N_IN_SLICES, axis=-1)
    ya = pool_mixer(a_in, pool_w, pool_scale) * jax.nn.silu(a_gate)
    yb = sgu_mixer(b_u, b_v, sgu_norm_g, sgu_w, sgu_b) * jax.nn.silu(b_gate)
    yc = fourier_mixer(c_in, fnet_w) * jax.nn.silu(c_gate)
    yd = na_mixer(d_q, d_k, d_v, na_rpb) * jax.nn.silu(d_gate)
    y = jnp.concatenate([ya, yb, yc, yd], axis=-1) @ w_out
    return x + gate[:, None] * y


def trunk(x, c, norm_g, w_ada, b_ada, w_in, w_out, pool_w, pool_scale,
          sgu_norm_g, sgu_w, sgu_b, fnet_w, na_rpb, final_norm_g):
    for l in range(DEPTH):
        x = mixer_layer(x, c, norm_g[l], w_ada[l], b_ada[l], w_in[l], w_out[l],
                        pool_w[l], pool_scale[l], sgu_norm_g[l], sgu_w[l], sgu_b[l],
                        fnet_w[l], na_rpb[l])
    return rmsnorm(x, final_norm_g)


def setup_inputs(seed: int = 0) -> dict:
    key = jax.random.key(seed)
    ks = jax.random.split(key, 17)
    f32 = jnp.float32
    n = lambda k, s: jax.random.normal(k, s, f32)
    return {
        "x_prompt": n(ks[0], (BATCH, SEQ, D_MODEL)),
        "x_sample": n(ks[1], (DEC_BATCH, DEC_SEQ, D_MODEL)),
        "c_prompt": n(ks[2], (BATCH, D_MODEL)),
        "c_sample": n(ks[3], (DEC_BATCH, D_MODEL)),
        "norm_g": 1.0 + 0.02 * n(ks[4], (DEPTH, D_MODEL)),
        "w_ada": n(ks[5], (DEPTH, D_MODEL, 3 * D_MODEL)) * (0.5 * D_MODEL ** -0.5),
        "b_ada": 0.02 * n(ks[6], (DEPTH, 3 * D_MODEL)),
        "w_in": n(ks[7], (DEPTH, D_MODEL, D_IN)) * D_MODEL ** -0.5,
        "w_out": n(ks[8], (DEPTH, D_MIX, D_MODEL)) * D_MIX ** -0.5,
        "pool_w": n(ks[9], (DEPTH, N_GROUPS, D_GROUP, D_GROUP)) * D_GROUP ** -0.5,
        "pool_scale": 1.0 + 0.02 * n(ks[10], (DEPTH, D_BRANCH)),
        "sgu_norm_g": 1.0 + 0.02 * n(ks[11], (DEPTH, D_BRANCH)),
        "sgu_w": n(ks[12], (DEPTH, N_GROUPS, CHUNK, CHUNK)) * CHUNK ** -0.5,
        "sgu_b": 1.0 + 0.02 * n(ks[13], (DEPTH, N_GROUPS, CHUNK)),
        "fnet_w": n(ks[14], (DEPTH, N_GROUPS, D_GROUP, D_GROUP)) * D_GROUP ** -0.5,
        "na_rpb": 0.1 * n(ks[15], (DEPTH, N_GROUPS, 2 * NA_KH_MAX - 1, 2 * NA_KW - 1)),
        "final_norm_g": 1.0 + 0.02 * n(ks[16], (D_MODEL,)),
    }


def reference(x_prompt, x_sample, c_prompt, c_sample, norm_g, w_ada, b_ada, w_in, w_out,
              pool_w, pool_scale, sgu_norm_g, sgu_w, sgu_b, fnet_w, na_rpb, final_norm_g):
    y_prompt = trunk(x_prompt, c_prompt, norm_g, w_ada, b_ada, w_in, w_out, pool_w, pool_scale,
                     sgu_norm_g, sgu_w, sgu_b, fnet_w, na_rpb, final_norm_g)
    y_sample = trunk(x_sample, c_sample, norm_g, w_ada, b_ada, w_in, w_out, pool_w, pool_scale,
                     sgu_norm_g, sgu_w, sgu_b, fnet_w, na_rpb, final_norm_g)
    return (y_prompt, y_sample)
```

```python
import sys
import numpy as np
import ml_dtypes
import concourse.bass as bass
import concourse.mybir as mybir
from concourse.bass_utils import run_bass_kernel_spmd

F32 = mybir.dt.float32
BF16 = mybir.dt.bfloat16
I32 = mybir.dt.int32
AF = mybir.ActivationFunctionType
ALU = mybir.AluOpType
NPBF = ml_dtypes.bfloat16

D = 1024
NEG = -240000.0
GELU_F = AF.Gelu_apprx_tanh


class Buf:
    __slots__ = ("name", "w", "r", "excl")

    def __init__(self, name, excl=False):
        self.name = name
        self.w = None
        self.r = []
        self.excl = excl


class Op:
    __slots__ = ("eng", "fn", "reads", "writes", "kind", "deps", "signal", "sem", "val", "clock", "idx", "inc", "tag")


class Prog:
    COMPUTE = ("pe", "act", "dve", "pool")

    def __init__(self, nc, n_dma_sems=28):
        self.nc = nc
        self.ops = []
        self.eng = {"pe": nc.tensor, "act": nc.scalar, "dve": nc.vector, "pool": nc.gpsimd, "sp": nc.sync}
        self.n_dma_sems = n_dma_sems

    def op(self, eng, fn, reads=(), writes=(), kind="c"):
        o = Op()
        o.eng, o.fn, o.kind = eng, fn, kind
        o.reads = [b for b in reads if b is not None and not b.excl]
        o.writes = [b for b in writes if b is not None] + [b for b in reads if b is not None and b.excl]
        o.idx = len(self.ops)
        f = sys._getframe(1)
        o.tag = (f.f_lineno, f.f_back.f_lineno if f.f_back else 0, f.f_back.f_back.f_lineno if f.f_back and f.f_back.f_back else 0)
        self.ops.append(o)
        return o

    def dma(self, out, in_, reads=(), writes=(), q="sp", **kw):
        e = self.eng[q]
        return self.op(q, lambda: e.dma_start(out=out, in_=in_, **kw), reads, writes, kind="dma")

    def barrier(self):
        self.op("sp", None, kind="bar")

    def finalize(self):
        nc = self.nc
        ops = self.ops
        for o in ops:
            o.clock = None
            o.signal = False
            o.sem = None
            if o.kind == "bar":
                o.deps = set()
                continue
            deps = set()
            for b in o.reads:
                if b.w is not None:
                    deps.add(b.w)
            for b in o.writes:
                if b.w is not None:
                    deps.add(b.w)
                deps.update(b.r)
            for b in o.reads:
                if o.kind in ("c", "reg"):
                    b.r = [r for r in b.r if not (ops[r].kind in ("c", "reg") and ops[r].eng == o.eng)]
                b.r.append(o.idx)
            for b in o.writes:
                b.w = o.idx
                b.r = []
            deps.discard(o.idx)
            if o.eng == "pe" and o.kind == "c":
                deps = {d for d in deps if not (ops[d].eng == "pe" and ops[d].kind == "c")}
            o.deps = deps
        dsems, dcnt, dlast, dnext = {}, {}, {}, {}
        for o in ops:
            if o.kind == "dma":
                q = o.eng
                if q not in dsems:
                    n = self.n_dma_sems if q == "sp" else 8
                    dsems[q] = [nc.alloc_semaphore(f"sem_d{q}{i}") for i in range(n)]
                    dcnt[q] = [0] * n
                    dlast[q] = [None] * n
                    dnext[q] = 0
                k = dnext[q]
                dnext[q] = (k + 1) % len(dsems[q])
                if dlast[q][k] is not None:
                    o.deps.add(dlast[q][k])
                dcnt[q][k] += 16
                dlast[q][k] = o.idx
                o.sem, o.val, o.inc = dsems[q][k], dcnt[q][k], 16
                o.signal = True
            elif o.kind == "cc":
                o.sem, o.val, o.inc = nc.alloc_semaphore(f"sem_cc{o.idx}"), 1, 1
                o.signal = True
        last_c = {}
        for o in ops:
            if o.kind == "bar":
                for idx in last_c.values():
                    ops[idx].signal = True
                continue
            for d in o.deps:
                ops[d].signal = True
            if o.kind == "c":
                last_c[o.eng] = o.idx
        for idx in last_c.values():
            ops[idx].signal = True
        esem = {e: nc.alloc_semaphore("sem_" + e) for e in self.COMPUTE}
        ecnt = {e: 0 for e in self.COMPUTE}
        for o in ops:
            if o.kind == "c" and o.signal:
                ecnt[o.eng] += 1
                o.sem, o.val, o.inc = esem[o.eng], ecnt[o.eng], 1
            elif o.kind == "reg":
                assert not o.signal, "register loads cannot be waited on"
        seen = {e: {} for e in self.eng}
        glob = {}
        ccglob = {}
        bar_clock = {}
        n_wait = 0
        for o in ops:
            if o.kind == "bar":
                bar_clock = dict(glob)
                continue
            e = self.eng[o.eng]
            sn = seen[o.eng]
            need = {}
            for d in o.deps:
                p = ops[d]
                k = id(p.sem)
                if sn.get(k, (None, 0))[1] < p.val and need.get(k, (None, 0))[1] < p.val:
                    need[k] = (p.sem, p.val)
            for k, sv in bar_clock.items():
                if sn.get(k, (None, 0))[1] < sv[1] and need.get(k, (None, 0))[1] < sv[1]:
                    need[k] = sv
            for d in o.deps:
                p = ops[d]
                if p.clock:
                    for k, sv in p.clock.items():
                        if k in need and need[k][1] <= sv[1] and k != id(p.sem):
                            pass
            for k, (s, v) in need.items():
                if sn.get(k, (None, 0))[1] < v:
                    e.wait_ge(s, v)
                    n_wait += 1
                    sn[k] = (s, v)
            for d in o.deps:
                p = ops[d]
                if p.clock:
                    for k, sv in p.clock.items():
                        if sn.get(k, (None, 0))[1] < sv[1]:
                            sn[k] = sv
            inst = o.fn()
            if o.signal:
                inst.then_inc(o.sem, o.inc)
                o.clock = dict(sn)
                o.clock[id(o.sem)] = (o.sem, o.val)
                tgt = ccglob if o.kind == "cc" else glob
                if tgt.get(id(o.sem), (None, 0))[1] < o.val:
                    tgt[id(o.sem)] = (o.sem, o.val)
        glob.update(ccglob)
        for k, (s, v) in glob.items():
            if seen["sp"].get(k, (None, 0))[1] < v:
                nc.sync.wait_ge(s, v)
        self.stats = dict(n_ops=len(ops), n_wait=n_wait, per_eng={e: sum(1 for o in ops if o.eng == e) for e in self.eng})


class Cfg:
    def __init__(self, NPT=16, NST=32, DEPTH=4, stop=None):
        self.NPT, self.NST, self.DEPTH = NPT, NST, DEPTH
        self.stop = stop
        self.dump = ()
        self.maxops = None
        self.Lp = 128 * NPT
        self.Lq = 128 * NST
        self.Ls = 4 * self.Lq
        self.N1p = NPT
        self.N1s = 4 * NST


def _na_geom(rows):
    kh = min(8, rows)
    r = np.arange(rows)
    rs = np.clip(r - kh // 2, 0, rows - kh)
    c = np.arange(64)
    cst = np.clip(c - 8, 0, 48)
    return kh, rs, cst


def na_mask_tile(rows, qt, kt):
    kh, rs, cst = _na_geom(rows)
    m = np.full((2, 64, 2, 64), NEG, np.float32)
    if kt < 0 or 2 * kt + 1 >= rows + 1 and 2 * kt >= rows:
        return m.reshape(128, 128)
    kc = np.arange(64)[:, None]
    qc = np.arange(64)[None, :]
    colok = (kc >= cst[qc]) & (kc < cst[qc] + 16)
    for kr in range(2):
        for qr in range(2):
            krow, qrow = 2 * kt + kr, 2 * qt + qr
            if krow < 0 or krow >= rows or qrow >= rows:
                continue
            if rs[qrow] <= krow < rs[qrow] + kh:
                m[kr, :, qr, :] = np.where(colok, 0.0, NEG)
    return m.reshape(128, 128)


def pool_band(L, gt, which, g):
    w = (2, 4, 8, 16)[g]
    out = np.zeros((128, 128), np.float32)
    for dst in range(128):
        t = 128 * gt + dst
        lo = min(max(t - w // 2, 0), L)
        hi = min(max(t - w // 2 + w, 0), L)
        cnt = hi - lo
        for s in range(lo, hi):
            sl = s - 128 * (gt + which)
            if 0 <= sl < 128:
                out[sl, dst] += 1.0 / cnt
        if which == 0:
            out[dst, dst] -= 1.0
    return out


def make_geometry(cfg):
    NPT, NST = cfg.NPT, cfg.NST
    rows_p, rows_s = 2 * NPT, 8 * NST
    geo = {}
    mask_tabs = [[] for _ in range(8)]
    slot_of = {}

    def add_shared(key, arr):
        if key not in slot_of:
            slot_of[key] = len(mask_tabs[0])
            for c in range(8):
                mask_tabs[c].append(arr)
        return slot_of[key]

    def add_percore(key, arrs):
        slot_of[key] = len(mask_tabs[0])
        for c in range(8):
            mask_tabs[c].append(arrs[c])
        return slot_of[key]

    interior = {d: na_mask_tile(64, 10, 10 + d) for d in range(-3, 4)}
    na_p = []
    for qt in range(NPT):
        lst = []
        for d in range(-3, 4):
            kt = qt + d
            if kt < 0 or kt >= NPT:
                continue
            m = na_mask_tile(rows_p, qt, kt)
            if (m == NEG).all():
                continue
            is_int = np.array_equal(m, interior[d])
            slot = add_shared(("int", d), m) if is_int else add_shared(("p", qt, d), m)
            lst.append((d, slot, is_int))
        na_p.append(lst)
    na_s = []
    for qt in range(NST):
        lst = []
        for d in range(-3, 4):
            kt = qt + d
            if kt < -2 or kt > NST + 1:
                continue
            ms = [na_mask_tile(rows_s, NST * (c % 4) + qt, NST * (c % 4) + kt) for c in range(8)]
            if all((m == NEG).all() for m in ms):
                continue
            if all(np.array_equal(m, interior[d]) for m in ms):
                slot = add_shared(("int", d), ms[0])
                lst.append((d, slot, True))
            else:
                slot = add_percore(("s", qt, d), ms)
                lst.append((d, slot, False))
        na_s.append(lst)
    geo["na_p"], geo["na_s"] = na_p, na_s
    geo["int_slots"] = {d: slot_of[("int", d)] for d in range(-3, 4) if ("int", d) in slot_of}
    geo["masks"] = [np.stack(t).astype(NPBF) for t in mask_tabs]
    band_tabs = [[] for _ in range(8)]
    bslot = {}

    def badd(key, arrs):
        if key not in bslot:
            bslot[key] = len(band_tabs[0])
            for c in range(8):
                band_tabs[c].append(arrs[c])
        return bslot[key]

    pool_p, pool_s = [], []
    for t in range(NPT):
        lst = []
        for g in range(4):
            for which in (-1, 0, 1):
                if t + which < 0 or t + which >= NPT:
                    continue
                m = pool_band(cfg.Lp, t, which, g)
                if not m.any():
                    continue
                mi = pool_band(128 * 64, 10, which, g)
                key = ("int", g, which) if np.array_equal(m, mi) else ("p", t, g, which)
                lst.append((g, which, badd(key, [m] * 8)))
        pool_p.append(lst)
    for t in range(NST):
        lst = []
        for g in range(4):
            for which in (-1, 0, 1):
                ms = [pool_band(cfg.Ls, NST * (c % 4) + t, which, g) for c in range(8)]
                if not any(m.any() for m in ms):
                    continue
                mi = pool_band(128 * 64, 10, which, g)
                if all(np.array_equal(m, mi) for m in ms):
                    key = ("int", g, which)
                else:
                    key = ("s", t, g, which)
                lst.append((g, which, badd(key, ms)))
        pool_s.append(lst)
    geo["pool_p"], geo["pool_s"] = pool_p, pool_s
    geo["bands"] = [np.stack(t).astype(NPBF) for t in band_tabs]
    return geo


def dft_consts(cfg):
    def cs(n):
        k = np.arange(n)
        ang = 2 * np.pi * np.outer(k, k) / n
        return np.cos(ang), np.sin(ang)
    c128, s128 = cs(128)
    out = {"c128": c128.astype(NPBF), "s128": s128.astype(NPBF)}
    for nm, n1, L in (("p", cfg.N1p, cfg.Lp), ("s", cfg.N1s, cfg.Ls)):
        c, s = cs(n1)
        out["cs1" + nm] = np.concatenate([c, -s], 1).astype(NPBF)
        out["cs2" + nm] = np.concatenate([s, c], 1).astype(NPBF)
        l2 = np.arange(128)[:, None]
        k1 = np.arange(n1)[None, :]
        ang = 2 * np.pi * l2 * k1 / L
        out["twc" + nm] = np.cos(ang).astype(np.float32)
        out["tws" + nm] = np.sin(ang).astype(np.float32)
    c64, s64 = cs(64)
    z = np.zeros((64, 64))
    out["bdc64"] = np.block([[c64, z], [z, c64]]).astype(np.float32)
    out["bds64"] = np.block([[-s64, z], [z, -s64]]).astype(np.float32)
    ecol = np.zeros((31, 64, 64), np.float32)
    for kc in range(64):
        for qc in range(64):
            b = kc - qc + 15
            if 0 <= b < 31:
                ecol[b, kc, qc] = 1.0
    out["ecol"] = ecol.reshape(31, 4096).astype(NPBF)
    out["ident_bf"] = np.eye(128, dtype=np.float32).astype(NPBF)
    out["ident_f"] = np.eye(128, dtype=np.float32)
    sel = np.zeros((2, 2, 128), np.float32)
    sel[0, 0, :] = 1.0
    sel[1, 1, :] = 1.0
    out["sel"] = sel.reshape(2, 256)
    return out


class TB:
    __slots__ = ("a", "b")

    def __init__(self, a, b):
        self.a, self.b = a, b


def build_program(cfg, geo, dbg=False):
    NPT, NST, DEPTH = cfg.NPT, cfg.NST, cfg.DEPTH
    Lp, Lq, Ls = cfg.Lp, cfg.Lq, cfg.Ls
    NM = geo["masks"][0].shape[0]
    NB = geo["bands"][0].shape[0]
    nc = bass.Bass("TRN2", target_bir_lowering=False)
    P = Prog(nc)
    act, dve, pool, pe, sp = nc.scalar, nc.vector, nc.gpsimd, nc.tensor, nc.sync

    def din(name, shape, dt=F32):
        return nc.dram_tensor(name, list(shape), dt, kind="ExternalInput").ap()

    def dscr(name, shape, dt=F32):
        return nc.dram_tensor(name, list(shape), dt).ap()

    xp_in = din("xp", [Lp, D])
    xs_in = din("xs", [Lq, D])
    xsh_in = din("xsh", [512, D])
    cT_in = din("cT", [128, 16])
    tab_in = din("tab", [1, 8], I32)
    w_ada = din("w_ada", [DEPTH, D, 3 * D])
    b_ada = din("b_ada", [DEPTH, 3 * D])
    ng_fm = din("norm_g_fm", [DEPTH, 128, 8])
    w_in = din("w_in", [DEPTH, D, 2816])
    w_inT = din("w_inT_ac", [DEPTH, 512, D])
    w_out = din("w_out", [DEPTH, D, D])
    pool_w = din("pool_w", [DEPTH, 4, 64, 64])
    pool_scale = din("pool_scale", [DEPTH, 256])
    sgu_ng = din("sgu_norm_g", [DEPTH, 256])
    sgu_wT = din("sgu_wT", [DEPTH, 4, 128, 128])
    sgu_bT = din("sgu_bT", [DEPTH, 128, 4])
    fnet_w = din("fnet_w", [DEPTH, 256, 64])
    rpbT = din("na_rpbT", [DEPTH, 31, 60])
    fin_g = din("final_norm_g", [1, D])
    k_ident_bf = din("ident_bf", [128, 128], BF16)
    k_ident_f = din("ident_f", [128, 128])
    k_c128 = din("c128", [128, 128], BF16)
    k_s128 = din("s128", [128, 128], BF16)
    k_cs = {n: din(n, [n1, 2 * n1], BF16) for n, n1 in (("cs1p", cfg.N1p), ("cs2p", cfg.N1p), ("cs1s", cfg.N1s), ("cs2s", cfg.N1s))}
    k_tw = {n: din(n, [128, n1]) for n, n1 in (("twcp", cfg.N1p), ("twsp", cfg.N1p), ("twcs", cfg.N1s), ("twss", cfg.N1s))}
    k_bdc = din("bdc64", [128, 128])
    k_bds = din("bds64", [128, 128])
    k_ecol = din("ecol", [31, 4096], BF16)
    k_sel = din("sel", [2, 256])
    k_masks = din("masks", [NM, 128, 128], BF16)
    k_bands = din("bands", [NB, 128, 128], BF16)

    yp_out = nc.dram_tensor("yp", [Lp, D], F32, kind="ExternalOutput").ap()
    ys_out = nc.dram_tensor("ys", [Lq, D], F32, kind="ExternalOutput").ap()

    xpd = dscr("xpd", [Lp, D])
    xsd = dscr("xsd", [Lq, D])
    p_out = dscr("p_out", [4 * Lq, 128], BF16)
    p_all = dscr("p_all", [16 * Lq, 128], BF16)
    pp = dscr("pp", [4 * Lp, 128], BF16)
    NH = 2 if Ls * 64 * 2 > (1 << 20) else 1
    assert NH == 1 or cfg.N1s == 128
    BLK = 128 // NH
    res_out = [dscr(f"res_out{h}", [Ls // NH, 64], BF16) for h in range(NH)]
    res_all = [dscr(f"res_all{h}", [4 * Ls // NH, 64], BF16) for h in range(NH)]
    res_p = dscr("res_p", [4 * Lp, 64], BF16)
    cg_s = dscr("cg_s", [Lq, 256], BF16)
    cg_p = dscr("cg_p", [Lp, 256], BF16)
    xb_out = [dscr(f"xb_out{i}", [256, D]) for i in range(2)]
    xb_all = [dscr(f"xb_all{i}", [1024, D]) for i in range(2)]
    cgb_out = [dscr(f"cgb_out{i}", [256, 256], BF16) for i in range(2)]
    cgb_all = [dscr(f"cgb_all{i}", [1024, 256], BF16) for i in range(2)]
    ud = dscr("ud", [60, 4096], BF16)
    res_own = [dscr(f"res_own{h}", [4 * Lq // NH, 64], BF16) for h in range(NH)]
    res_halo = [dscr(f"res_halo{h}", [4 * 512 // NH, 64], BF16) for h in range(NH)]
    xb_halo = dscr("xb_halo", [512, D])
    cgb_halo = dscr("cgb_halo", [512, 256], BF16)
    gated = dscr("gated", [DEPTH, 2, D])
    dbuf = {}

    def DB(name):
        if name not in dbuf:
            dbuf[name] = Buf(name)
        return dbuf[name]

    sb_used = [0]

    def salloc(name, shape, dt):
        t = nc.alloc_sbuf_tensor(name, list(shape), dt)
        n = int(np.prod(shape[1:])) * (4 if dt in (F32, I32) else 2)
        sb_used[0] += n
        return TB(t, Buf(name))

    W_eff = salloc("W_eff", [128, 8, 3072], BF16)
    wo_abd = salloc("wo_abd", [128, 6, 1024], BF16)
    wo_c = salloc("wo_c", [128, 2, 1024], BF16)
    wst = salloc("wst", [128, 2, 1024], F32)
    gbc = salloc("gbc", [128, 1024], F32)
    ident_bf = salloc("ident_bf_s", [128, 128], BF16)
    ident_f = salloc("ident_f_s", [128, 128], F32)
    c128 = salloc("c128_s", [128, 128], BF16)
    s128 = salloc("s128_s", [128, 128], BF16)
    cs_sb = {n: salloc(n + "_s", [n1, 2 * n1], BF16) for n, n1 in (("cs1p", cfg.N1p), ("cs2p", cfg.N1p), ("cs1s", cfg.N1s), ("cs2s", cfg.N1s))}
    tw_sb = {n: salloc(n + "_s", [128, n1], F32) for n, n1 in (("twcp", cfg.N1p), ("twsp", cfg.N1p), ("twcs", cfg.N1s), ("twss", cfg.N1s))}
    masks = salloc("masks_s", [128, NM, 128], BF16)
    bands = salloc("bands_s", [128, NB, 128], BF16)
    Gt = salloc("Gt", [128, 28, 128], BF16)
    Bint = salloc("Bint", [128, 20, 128], BF16)
    gsT = salloc("gsT", [128, DEPTH * 16], F32)
    shT = salloc("shT", [128, DEPTH * 16], F32)
    sguW = salloc("sguW", [128, 4, 128], BF16)
    sguB = salloc("sguB", [128, 4], F32)
    sng = salloc("sng", [128, 256], F32)
    gfin = salloc("gfin", [128, 1024], F32)
    tabs = salloc("tabs", [1, 8], I32)
    st_dummy = salloc("st_dummy", [1, 8], F32)
    BintB = [Buf(f"Bint{i}") for i in range(20)]
    WB = [[Buf(f"W{kc}_{g}") for g in range(12)] for kc in range(8)]

    def wb(kc, c0, n):
        return [WB[kc][g] for g in range(c0 // 256, (c0 + n - 1) // 256 + 1)]
    GB = [Buf(f"Gt{i}") for i in range(28)]
    ARENA = 48 * 1024
    arena = nc.alloc_sbuf_tensor("arena", [128, ARENA], BF16)
    sb_used[0] += ARENA * 2
    ar_off = [0]

    def carve(name, shape, dt, parts=128):
        n = int(np.prod(shape[1:]))
        ne = n * (2 if dt in (F32, I32) else 1)
        o = ar_off[0]
        o = (o + 1) // 2 * 2
        assert o + ne <= ARENA, f"arena overflow at {name}: {o + ne} > {ARENA}"
        v = arena[0:shape[0], o:o + ne]
        if dt in (F32, I32):
            v = v.bitcast(dt)
        if len(shape) == 3:
            v = v.rearrange("p (a b) -> p a b", b=shape[2])
        elif len(shape) == 4:
            v = v.rearrange("p (a b c) -> p a b c", b=shape[2], c=shape[3])
        ar_off[0] = o + ne
        return TB(v, Buf(name))

    def arena_reset():
        ar_off[0] = 0

    banks = [TB(nc.alloc_psum_tensor(f"bank{i}", [128, 512], F32), Buf(f"bank{i}", excl=True)) for i in range(8)]
    bank_rr = [0]
    pinned = set()

    def bank():
        while True:
            i = bank_rr[0] % 8
            bank_rr[0] += 1
            if i not in pinned:
                return banks[i]

    def A(eng, func, out, in_, reads, writes, **kw):
        e = P.eng[eng]
        return P.op(eng, lambda: e.activation(out=out, in_=in_, func=func, **kw), reads, writes)

    def CP(eng, out, in_, reads, writes):
        e = P.eng[eng]
        if eng == "act":
            return P.op(eng, lambda: e.activation(out=out, in_=in_, func=AF.Copy), reads, writes)
        return P.op(eng, lambda: e.tensor_copy(out=out, in_=in_), reads, writes)

    def TT(eng, out, in0, in1, op, reads, writes):
        e = P.eng[eng]
        return P.op(eng, lambda: e.tensor_tensor(out=out, in0=in0, in1=in1, op=op), reads, writes)

    def TS(eng, out, in0, s1, s2, op0, op1, reads, writes):
        e = P.eng[eng]
        if s2 is None:
            return P.op(eng, lambda: e.tensor_scalar(out=out, in0=in0, scalar1=s1, scalar2=None, op0=op0), reads, writes)
        return P.op(eng, lambda: e.tensor_scalar(out=out, in0=in0, scalar1=s1, scalar2=s2, op0=op0, op1=op1), reads, writes)

    def STT(eng, out, in0, scalar, in1, op0, op1, reads, writes):
        e = P.eng[eng]
        return P.op(eng, lambda: e.scalar_tensor_tensor(out=out, in0=in0, scalar=scalar, in1=in1, op0=op0, op1=op1), reads, writes)

    def MM(out, lhsT, rhs, start, stop, reads, writes):
        return P.op("pe", lambda: pe.matmul(out, lhsT, rhs, start=start, stop=stop), reads, writes)

    def TR(out, in_, ident, reads, writes):
        return P.op("pe", lambda: pe.transpose(out, in_, ident), reads, writes)

    def MS(eng, out, val, writes):
        e = P.eng[eng]
        return P.op(eng, lambda: e.memset(out, val), (), writes)

    for tb, src in ((ident_bf, k_ident_bf), (ident_f, k_ident_f), (c128, k_c128), (s128, k_s128)):
        P.dma(tb.a[:], src, (), [tb.b])
    for n in cs_sb:
        P.dma(cs_sb[n].a[:], k_cs[n], (), [cs_sb[n].b])
    for n in tw_sb:
        P.dma(tw_sb[n].a[:], k_tw[n], (), [tw_sb[n].b])
    P.dma(masks.a[:], k_masks.rearrange("m p q -> p m q"), (), [masks.b])
    P.dma(bands.a[:], k_bands.rearrange("m p q -> p m q"), (), [bands.b])
    P.dma(gfin.a[:], fin_g.to_broadcast([128, D]), (), [gfin.b])
    P.dma(tabs.a[:], tab_in, (), [tabs.b])
    regs = [sp.alloc_register(f"dyn{i}") for i in range(6)]
    dyn = []

    def _ld(i):
        return lambda: sp.reg_load(regs[i], tabs.a[0:1, i:i + 1])
    for i in range(6):
        P.op("sp", _ld(i), [tabs.b] if i == 0 else (), (), kind="reg")
    hi = [12 * Lq, 3 * Lq // NH, (Ls - 256) // NH, (Ls - 256) // NH, 768, 768]

    class _Dyn:
        def __init__(self):
            self.v = [None] * 6

        def get(self, i):
            if self.v[i] is None:
                self.v[i] = sp.snap(regs[i], min_val=0, max_val=hi[i])
            return self.v[i]
    dynv = _Dyn()

    GROUPS = [[0, 1, 2, 3], [4, 5, 6, 7]]

    def dsl(ap, i, add, n):
        return (ap[add:] if add else ap)[bass.ds(dynv.get(i), n)]

    arena_reset()
    wa32 = [carve(f"wa32_{i}", [128, 3072], F32) for i in range(4)]
    wa16 = [carve(f"wa16_{i}", [128, 3072], BF16) for i in range(2)]
    cT32 = carve("cT32", [128, 16], F32)
    sc16 = carve("sc16", [128, 16], BF16)
    modrow = carve("modrow", [2, 3072], F32)
    brow = carve("brow", [2, 3072], F32)
    ng_sb = carve("ng_sb", [128, 8], F32)
    tmp16 = carve("tmp16", [128, 16], F32)
    sel_sb = carve("sel_sb", [2, 256], F32)
    P.dma(cT32.a[:], cT_in, (), [cT32.b])
    A("act", AF.Silu, sc16.a[:], cT32.a[:], [cT32.b], [sc16.b])
    for l in range(DEPTH):
        mb = banks[0:6]
        for kc in range(8):
            w32, w16 = wa32[(l * 8 + kc) % 4], wa16[kc % 2]
            P.dma(w32.a[:], w_ada[l, kc * 128:(kc + 1) * 128, :], (), [w32.b])
            CP("pool", w16.a[:, 0:1024], w32.a[:, 0:1024], [w32.b], [w16.b])
            CP("dve", w16.a[:, 1024:2048], w32.a[:, 1024:2048], [w32.b], [w16.b])
            CP("act", w16.a[:, 2048:3072], w32.a[:, 2048:3072], [w32.b], [w16.b])
            for nb in range(6):
                MM(mb[nb].a[0:2, :], sc16.a[:, kc * 2:(kc + 1) * 2], w16.a[:, nb * 512:(nb + 1) * 512],
                   kc == 0, kc == 7, [sc16.b, w16.b], [mb[nb].b])
        P.dma(brow.a[:], b_ada[l:l + 1, :].to_broadcast([2, 3 * D]), (), [brow.b])
        for nb in range(6):
            TT("dve", modrow.a[0:2, nb * 512:(nb + 1) * 512], mb[nb].a[0:2, :], brow.a[0:2, nb * 512:(nb + 1) * 512],
               ALU.add, [mb[nb].b, brow.b], [modrow.b])
        P.dma(gated[l], modrow.a[0:2, 2048:3072], [modrow.b], [DB("gated")])
        bt = banks[6]
        for ch in range(16):
            TR(bt.a[:, ch * 2:(ch + 1) * 2], modrow.a[0:2, ch * 128:(ch + 1) * 128], ident_f.a[0:2, 0:2],
               [modrow.b, ident_f.b], [bt.b])
        P.dma(ng_sb.a[:], ng_fm[l], (), [ng_sb.b])
        CP("dve", shT.a[:, l * 16:(l + 1) * 16], bt.a[:, 0:16], [bt.b], [shT.b])
        TS("dve", tmp16.a[:], bt.a[:, 16:32], 1.0, None, ALU.add, None, [bt.b], [tmp16.b])
        TT("dve", gsT.a[:, l * 16:(l + 1) * 16].rearrange("p (k s) -> p k s", s=2),
           tmp16.a[:].rearrange("p (k s) -> p k s", s=2),
           ng_sb.a[:].unsqueeze(2).to_broadcast([128, 8, 2]), ALU.mult, [tmp16.b, ng_sb.b], [gsT.b])
    P.barrier()

    def layer_prep(l):
        arena_reset()
        wi32 = [carve(f"wi32_{i}", [128, 2816], F32) for i in range(4)]
        wT32 = carve("wT32", [128, 4, 1024], F32)
        BDa = carve("BDa", [128, 2, 256], F32)
        BDr = carve("BDr", [128, 2, 256], F32)
        BDi = carve("BDi", [128, 2, 256], F32)
        psb = carve("psb", [128, 256], F32)
        fws = carve("fws", [128, 2, 64], F32)
        bdc = carve("bdc", [128, 128], F32)
        bds = carve("bds", [128, 128], F32)
        ecol = carve("ecol", [31, 4096], BF16)
        rpb32 = carve("rpb32", [31, 60], F32)
        rpb16 = carve("rpb16", [31, 60], BF16)
        U_sb = carve("U_sb", [60, 4096], BF16)
        sgw32 = carve("sgw32", [128, 4, 128], F32)
        P.dma(sgw32.a[:], sgu_wT[l].rearrange("h q p -> q h p"), (), [sgw32.b])
        CP("dve", sguW.a[:], sgw32.a[:], [sgw32.b], [sguW.b])
        P.dma(sguB.a[:], sgu_bT[l], (), [sguB.b])
        P.dma(sng.a[:], sgu_ng[l:l + 1, :].to_broadcast([128, 256]), (), [sng.b])
        MS("dve", BDa.a[:], 0.0, [BDa.b])
        MS("dve", BDr.a[:], 0.0, [BDr.b])
        MS("dve", BDi.a[:], 0.0, [BDi.b])
        for g in range(4):
            P.dma(BDa.a[(g % 2) * 64:(g % 2 + 1) * 64, g // 2, g * 64:(g + 1) * 64], pool_w[l, g], (), [BDa.b])
        P.dma(psb.a[:], pool_scale[l:l + 1, :].to_broadcast([128, 256]), (), [psb.b])
        for cc in range(2):
            TT("dve", BDa.a[:, cc, :], BDa.a[:, cc, :], psb.a[:], ALU.mult, [BDa.b, psb.b], [BDa.b])
        P.dma(fws.a[:], fnet_w[l].rearrange("(c p) d -> p c d", c=2), (), [fws.b])
        P.dma(bdc.a[:], k_bdc, (), [bdc.b])
        P.dma(bds.a[:], k_bds, (), [bds.b])
        for cc in range(2):
            bx = bank()
            MM(bx.a[:, 0:64], bdc.a[:], fws.a[:, cc, :], True, True, [bdc.b, fws.b], [bx.b])
            MM(bx.a[:, 64:128], bds.a[:], fws.a[:, cc, :], True, True, [bds.b, fws.b], [bx.b])
            for gl in range(2):
                g = 2 * cc + gl
                CP("act", BDr.a[gl * 64:(gl + 1) * 64, cc, g * 64:(g + 1) * 64], bx.a[gl * 64:(gl + 1) * 64, 0:64], [bx.b], [BDr.b])
                CP("act", BDi.a[gl * 64:(gl + 1) * 64, cc, g * 64:(g + 1) * 64], bx.a[gl * 64:(gl + 1) * 64, 64:128], [bx.b], [BDi.b])
        P.dma(wT32.a[:], w_inT[l].rearrange("(j p) r -> p j r", j=4), (), [wT32.b])
        copies = [(256, 512, 1536, 0.5), (512, 1024, 512, 1.0), (1024, 1280, 1792, 0.5), (1536, 1792, 2048, 0.5),
                  (1792, 2304, 2560, 1.0), (2304, 2560, 256, 1.0), (2560, 2816, 2304, 0.5)]
        engs = ["dve", "dve", "dve", "act", "dve", "dve", "act"]
        for kc in range(8):
            wi = wi32[kc % 4]
            P.dma(wi.a[:], w_in[l, kc * 128:(kc + 1) * 128, :], (), [wi.b])
            for (s0, s1, d0, sc), en in zip(copies, engs):
                if en == "act":
                    A("act", AF.Identity, W_eff.a[:, kc, d0:d0 + (s1 - s0)], wi.a[:, s0:s1], [wi.b], wb(kc, d0, s1 - s0), scale=sc)
                else:
                    TS(en, W_eff.a[:, kc, d0:d0 + (s1 - s0)], wi.a[:, s0:s1], sc, 0.0, ALU.mult, ALU.add, [wi.b], wb(kc, d0, s1 - s0))
            ba = bank()
            for cc in range(2):
                MM(ba.a[:, 0:256], wT32.a[:, cc, kc * 128:(kc + 1) * 128], BDa.a[:, cc, :], cc == 0, cc == 1,
                   [wT32.b, BDa.b], [ba.b])
            CP("act", W_eff.a[:, kc, 0:256], ba.a[:, 0:256], [ba.b], wb(kc, 0, 256))
            bc = bank()
            for cc in range(2):
                MM(bc.a[:, 0:256], wT32.a[:, 2 + cc, kc * 128:(kc + 1) * 128], BDr.a[:, cc, :], cc == 0, cc == 1,
                   [wT32.b, BDr.b], [bc.b])
            for cc in range(2):
                MM(bc.a[:, 256:512], wT32.a[:, 2 + cc, kc * 128:(kc + 1) * 128], BDi.a[:, cc, :], cc == 0, cc == 1,
                   [wT32.b, BDi.b], [bc.b])
            CP("dve", W_eff.a[:, kc, 1024:1536].rearrange("p (g r d) -> p g r d", g=4, r=2),
               bc.a[:, :].rearrange("p (r g d) -> p g r d", r=2, g=4), [bc.b], wb(kc, 1024, 512))
        P.dma(ecol.a[:], k_ecol, (), [ecol.b])
        P.dma(rpb32.a[:], rpbT[l], (), [rpb32.b])
        CP("dve", rpb16.a[:], rpb32.a[:], [rpb32.b], [rpb16.b])
        for ch in range(8):
            bu = bank()
            MM(bu.a[0:60, :], rpb16.a[:], ecol.a[:, ch * 512:(ch + 1) * 512], True, True, [rpb16.b, ecol.b], [bu.b])
            A("act", AF.Identity, U_sb.a[:, ch * 512:(ch + 1) * 512], bu.a[0:60, :], [bu.b], [U_sb.b], scale=8.0)
        P.dma(ud, U_sb.a[:], [U_sb.b], [DB("ud")])
        MS("dve", Gt.a[:], 0.0, GB)
        udv = ud.rearrange("r (k q) -> r k q", k=64)
        GB4 = {(i, kr, qr): Buf(f"Gt{i}_{kr}{qr}") for i in range(28) for kr in range(2) for qr in range(2)}
        P.op("dve", lambda: dve.memset(st_dummy.a[:], 0.0), GB, [GB4[k] for k in GB4])
        for h in range(4):
            for kr in range(2):
                for qr in range(2):
                    a0 = 2 * (-3) + kr - qr + 7
                    assert 0 <= a0 and a0 + 12 <= 14
                    P.dma(Gt.a[kr * 64:(kr + 1) * 64, h * 7:h * 7 + 7, qr * 64:(qr + 1) * 64],
                          udv[h * 15 + a0:h * 15 + a0 + 13:2].rearrange("d k q -> k d q"),
                          [DB("ud")], [GB4[(h * 7 + dd, kr, qr)] for dd in range(7)])
        for h in range(4):
            for d in range(-2, 3):
                if d in geo["int_slots"]:
                    gi_ = h * 7 + d + 3
                    TT("dve", Bint.a[:, h * 5 + d + 2, :], Gt.a[:, gi_, :],
                       masks.a[:, geo["int_slots"][d], :], ALU.add, [GB4[(gi_, kr, qr)] for kr in range(2) for qr in range(2)] + [masks.b], [BintB[h * 5 + d + 2]])
        gb1 = carve("gb1", [128, 1024], F32)
        gb2 = carve("gb2", [128, 1024], F32)
        wout_prep(l, 1, slots=[(wi32[i].a[:, 0:1024], wi32[i].b) for i in range(4)], gates=[gb1, gb2])
        for i in range(28):
            P.op("dve", lambda: dve.memset(st_dummy.a[:], 0.0), [GB4[(i, kr, qr)] for kr in range(2) for qr in range(2)], [GB[i]])
        P.barrier()

    def wout_prep(l, s, slots=None, gates=None):
        if slots is None:
            slots = [(wst.a[:, i, :], wstb[i]) for i in range(2)]
        if gates is None:
            gates = [gbc, gbc]
        k = 0
        if l >= 1:
            g_ = gates[0]
            P.dma(g_.a[:], gated[l - 1, s:s + 1, :].to_broadcast([128, D]), [DB("gated")], [g_.b])
            for i, kc in enumerate((4, 5)):
                sa, sb_ = slots[k % len(slots)]
                P.dma(sa, w_out[l - 1, kc * 128:(kc + 1) * 128, :], (), [sb_])
                TT("dve", wo_c.a[:, i, :], sa, g_.a[:], ALU.mult, [sb_, g_.b], [wo_c.b])
                k += 1
        if l < DEPTH:
            g_ = gates[1]
            P.dma(g_.a[:], gated[l, s:s + 1, :].to_broadcast([128, D]), [DB("gated")], [g_.b])
            for i, kc in enumerate((0, 1, 2, 3, 6, 7)):
                sa, sb_ = slots[k % len(slots)]
                P.dma(sa, w_out[l, kc * 128:(kc + 1) * 128, :], (), [sb_])
                TT("dve", wo_abd.a[:, i, :], sa, g_.a[:], ALU.mult, [sb_, g_.b], [wo_abd.b])
                k += 1
    wstb = [Buf("wst0"), Buf("wst1")]
    def tile_layout():
        arena_reset()
        L = {}
        L["x"] = [carve(f"x{i}", [128, 1024], F32) for i in range(6)]
        L["k"] = [carve(f"k{i}", [128, 2, 128], BF16) for i in range(8)]
        L["v"] = [carve(f"v{i}", [128, 4, 65], BF16) for i in range(8)]
        L["a"] = [carve(f"a{i}", [128, 256], BF16) for i in range(8)]
        L["q"] = [carve(f"q{i}", [128, 2, 128], BF16) for i in range(5)]
        L["yb"] = [carve(f"yb{i}", [128, 256], BF16) for i in range(5)]
        L["gAB"] = [carve(f"gAB{i}", [128, 512], BF16) for i in range(5)]
        L["gCD"] = [carve(f"gCD{i}", [128, 512], BF16) for i in range(5)]
        for nm in ("sq", "xc", "th", "tha", "zca"):
            L[nm] = carve(nm, [128, 512], F32)
        for nm, shp, dt in (("fo", [128, 256], BF16), ("cg", [128, 256], BF16), ("ycb", [128, 256], BF16),
                            ("ycT", [128, 2, 128], BF16), ("xhat", [128, 1024], BF16), ("hT", [128, 8, 128], BF16),
                            ("ugf", [128, 512], F32), ("pbuf", [128, 512], BF16), ("qkt", [128, 512], BF16),
                            ("st", [128, 16], F32), ("vnb", [128, 256], BF16), ("ugg", [128, 256], F32),
                            ("ycat", [128, 512], BF16), ("ycatT", [128, 6, 128], BF16), ("et", [128, 6, 128], BF16),
                            ("nrm", [128, 8], F32)):
            L[nm] = [carve(f"{nm}{i}", shp, dt) for i in range(2)]
        L["junk"] = carve("junk", [128, 1024], BF16)
        return L

    def bfview(bk):
        return bk.a[:].bitcast(BF16).rearrange("p (c t) -> p c t", t=128)

    def tile_phase(seg, l):
        L = tile_layout()
        s = 0 if seg == "p" else 1
        nt = NPT if seg == "p" else NST
        tiles = list(range(nt)) if seg == "p" else list(range(-2, nt + 2))
        final = (l == DEPTH)
        for v in L["v"]:
            MS("dve", v.a[:], 1.0, [v.b])
        xsrc0 = xp_in if seg == "p" else xs_in
        xd = xpd if seg == "p" else xsd
        cgd = cg_p if seg == "p" else cg_s
        pd = pp if seg == "p" else p_out
        yout = yp_out if seg == "p" else ys_out
        Lseg = Lp if seg == "p" else Lq
        resv_p = res_p.rearrange("(g t) c -> t g c", g=4)
        resv_s = [r_.rearrange("(g t) c -> t g c", g=4) for r_ in res_own]
        resv_h = [r_.rearrange("(g t) c -> t g c", g=4) for r_ in res_halo]
        pdv = pd.rearrange("(g t) v -> t g v", g=4)
        nal = geo["na_p"] if seg == "p" else geo["na_s"]
        pol = geo["pool_p"] if seg == "p" else geo["pool_s"]
        rec = {}
        cnt = {"x": 0, "ld": 0, "s2": 0, "q": 0}

        def halo(t):
            return seg == "s" and (t < 0 or t >= nt)

        def S1_load(t):
            par = cnt["ld"] % 2
            cnt["ld"] += 1
            hl = halo(t)
            x = L["x"][cnt["x"] % 6]
            cnt["x"] += 1
            r = {"x": x, "par": par}
            rec[t] = r
            rows = slice(128 * t, 128 * t + 128)
            xbuf = DB(f"x{seg}{t}")
            if l == 0:
                if hl:
                    hi_ = (t + 2) if t < 0 else (2 + t - nt)
                    P.dma(x.a[:], xsh_in[hi_ * 128:(hi_ + 1) * 128, :], (), [x.b])
                else:
                    P.dma(x.a[:], xsrc0[rows, :], (), [x.b])
            elif hl:
                hi_ = (t + 2) if t < 0 else (2 + t - nt)
                P.dma(x.a[:], xb_halo[hi_ * 128:(hi_ + 1) * 128, :], [DB("xb_halo")], [x.b])
            else:
                P.dma(x.a[:], xd[rows, :], [xbuf], [x.b])
            if l >= 1:
                fo, cg = L["fo"][par], L["cg"][par]
                fov = fo.a[:].rearrange("p (g c) -> p g c", g=4)
                if seg == "p":
                    P.dma(fov, resv_p[rows, :, :], [DB("res_p")], [fo.b])
                    P.dma(cg.a[:], cgd[rows, :], [DB(f"cg{seg}{t}")], [cg.b])
                else:
                    if hl:
                        hi_ = (t + 2) if t < 0 else (2 + t - nt)
                        hr = slice(hi_ * 128, (hi_ + 1) * 128)
                        P.dma(cg.a[:], cgb_halo[hr, :], [DB("cgb_halo")], [cg.b])
                        for h in range(NH):
                            P.dma(fov[h * BLK:(h + 1) * BLK], resv_h[h][hi_ * BLK:(hi_ + 1) * BLK, :, :], [DB("res_halo")], [fo.b])
                    else:
                        P.dma(cg.a[:], cgd[rows, :], [DB(f"cg{seg}{t}")], [cg.b])
                        for h in range(NH):
                            P.dma(fov[h * BLK:(h + 1) * BLK], resv_s[h][t * BLK:(t + 1) * BLK, :, :], [DB("res_own")], [fo.b])

        def S1(t, part):
            r = rec[t]
            par, x = r["par"], r["x"]
            hl = halo(t)
            rows = slice(128 * t, 128 * t + 128)
            st = L["st"][par]
            junk = L["junk"]
            xhat, hT = L["xhat"][par], L["hT"][par]
            hTb = hTbufs[par]
            if part in ("a", "ac", "an"):
                S1a(t, r, par, x, hl, rows, st, junk, xhat, {"a": "both", "ac": "c", "an": "n"}[part])
            elif part == "t" and not final:
                bte, bto = bank(), bank()
                for c in range(8):
                    bt = bte if c % 2 == 0 else bto
                    TR(bfview(bt)[:, c // 2, :], xhat.a[:, c * 128:(c + 1) * 128], ident_bf.a[:], [xhat.b, ident_bf.b], [bt.b])
                for c in range(8):
                    col = l * 16 + c * 2 + s
                    if c % 2 == 0:
                        TS("dve", hT.a[:, c, :], bfview(bte)[:, c // 2, :], gsT.a[:, col:col + 1], shT.a[:, col:col + 1], ALU.mult, ALU.add,
                           [bte.b, gsT.b, shT.b], [hTb[c]])
                    else:
                        A("act", AF.Identity, hT.a[:, c, :], bfview(bto)[:, c // 2, :], [bto.b, gsT.b, shT.b], [hTb[c]],
                          scale=gsT.a[:, col:col + 1], bias=shT.a[:, col:col + 1])
            elif part == "m" and not final:
                S1m(t, r, par, x, hl, rows, st, hT, hTb)

        def S1a(t, r, par, x, hl, rows, st, junk, xhat, which):
            if l >= 1 and which in ("both", "c"):
                fo, cg, ycb, ycT = L["fo"][par], L["cg"][par], L["ycb"][par], L["ycT"][par]
                TT("dve", ycb.a[:], fo.a[:], cg.a[:], ALU.mult, [fo.b, cg.b], [ycb.b])
                bt = bank()
                btv = bfview(bt)
                for c in range(2):
                    TR(btv[:, c, :], ycb.a[:, c * 128:(c + 1) * 128], ident_bf.a[:], [ycb.b, ident_bf.b], [bt.b])
                CP("act", ycT.a[:], btv[:, 0:2, :], [bt.b], [ycT.b])
                for nb in range(2):
                    bw = bank()
                    for c in range(2):
                        MM(bw.a[:, :], ycT.a[:, c, :], wo_c.a[:, c, nb * 512:(nb + 1) * 512], c == 0, c == 1, [ycT.b, wo_c.b], [bw.b])
                    TT("dve", x.a[:, nb * 512:(nb + 1) * 512], bw.a[:, :], x.a[:, nb * 512:(nb + 1) * 512], ALU.add, [bw.b, x.b], [x.b])
            if which == "c":
                return

            def rsq(yc, vc, tc):
                y, v, tm = st.a[:, yc:yc + 1], st.a[:, vc:vc + 1], st.a[:, tc:tc + 1]
                TS("dve", y.bitcast(I32), v.bitcast(I32), 1, None, ALU.arith_shift_right, None, [st.b], [st.b])
                TS("dve", y.bitcast(I32), y.bitcast(I32), -1, 0x5f3759df, ALU.mult, ALU.add, [st.b], [st.b])
                for _ in range(2):
                    STT("dve", tm, y, v, y, ALU.mult, ALU.mult, [st.b], [st.b])
                    STT("dve", tm, tm, -0.5, y, ALU.mult, ALU.mult, [st.b], [st.b])
                    STT("dve", y, y, 1.5, tm, ALU.mult, ALU.add, [st.b], [st.b])

            A("act", AF.Square, junk.a[:], x.a[:], [x.b], [junk.b, st.b], accum_out=st.a[:, 0:1])
            TS("dve", st.a[:, 1:2], st.a[:, 0:1], 1.0 / D, 1e-6, ALU.mult, ALU.add, [st.b], [st.b])
            rsq(2, 1, 3)
            if final:
                if hl:
                    return
                A("act", AF.Identity, x.a[:], x.a[:], [x.b, st.b], [x.b], scale=st.a[:, 2:3])
                TT("pool", x.a[:], x.a[:], gfin.a[:], ALU.mult, [x.b, gfin.b], [x.b])
                P.dma(yout[rows, :], x.a[:], [x.b], [DB(f"y{seg}{t}")])
                return
            A("act", AF.Identity, xhat.a[:], x.a[:], [x.b, st.b], [xhat.b], scale=st.a[:, 2:3])

        def S1m(t, r, par, x, hl, rows, st, hT, hTb):
            def rsq(yc, vc, tc):
                y, v, tm = st.a[:, yc:yc + 1], st.a[:, vc:vc + 1], st.a[:, tc:tc + 1]
                TS("dve", y.bitcast(I32), v.bitcast(I32), 1, None, ALU.arith_shift_right, None, [st.b], [st.b])
                TS("dve", y.bitcast(I32), y.bitcast(I32), -1, 0x5f3759df, ALU.mult, ALU.add, [st.b], [st.b])
                for _ in range(2):
                    STT("dve", tm, y, v, y, ALU.mult, ALU.mult, [st.b], [st.b])
                    STT("dve", tm, tm, -0.5, y, ALU.mult, ALU.mult, [st.b], [st.b])
                    STT("dve", y, y, 1.5, tm, ALU.mult, ALU.add, [st.b], [st.b])
            ks = L["k"][(t + 2) % 8]
            vs = L["v"][(t + 2) % 8]
            as_ = L["a"][(t + 2) % 8]

            def mm_tm(col0, n):
                bk = bank()
                for kc in range(8):
                    MM(bk.a[:, 0:n], hT.a[:, kc, :], W_eff.a[:, kc, col0:col0 + n], kc == 0, kc == 7, [hTb[kc]] + wb(kc, col0, n), [bk.b])
                return bk

            def mm_fm(ms):
                bk = bank()
                for m in ms:
                    for kc in range(8):
                        MM(bk.a[:, m * 128:(m + 1) * 128], W_eff.a[:, kc, 2560 + m * 128:2560 + (m + 1) * 128], hT.a[:, kc, :],
                           kc == 0, kc == 7, [hTb[kc]] + wb(kc, 2560 + m * 128, 128), [bk.b])
                return bk
            if hl:
                b0 = mm_tm(0, 512)
                CP("dve", as_.a[:], b0.a[:, 0:256], [b0.b], [as_.b])
                CP("dve", vs.a[:, :, 0:64], b0.a[:, 256:512].rearrange("p (h c) -> p h c", h=4), [b0.b], [vs.b])
                qkt = L["qkt"][par]
                bq = mm_tm(2816, 256)
                CP("act", qkt.a[:, 256:512], bq.a[:, 0:256], [bq.b], [qkt.b])
                bt = bank()
                for c in range(2):
                    TR(bfview(bt)[:, c, :], qkt.a[:, 256 + c * 128:256 + (c + 1) * 128], ident_bf.a[:], [qkt.b, ident_bf.b], [bt.b])
                CP("act", ks.a[:], bfview(bt)[:, 0:2, :], [bt.b], [ks.b])
                return
            qi = cnt["q"] % 5
            cnt["q"] += 1
            qs, gAB, gCD, yb = L["q"][qi], L["gAB"][qi], L["gCD"][qi], L["yb"][qi]
            r.update(q=qs, gAB=gAB, gCD=gCD, yb=yb)
            ugf, pbuf, vnb, ugg = L["ugf"][par], L["pbuf"][par], L["vnb"][par], L["ugg"][par]
            sq, xc, th, tha, zca = L["sq"], L["xc"], L["th"], L["tha"], L["zca"]
            qkt = L["qkt"][par]
            bq = mm_tm(2560, 512)
            CP("act", qkt.a[:], bq.a[:, :], [bq.b], [qkt.b])
            b1 = mm_tm(512, 512)
            A("act", AF.Square, sq.a[:], b1.a[:, :], [b1.b], [sq.b], scale=0.21145921592587737)
            A("act", AF.Copy, xc.a[:], b1.a[:, :], [b1.b], [xc.b])
            STT("dve", sq.a[:], sq.a[:], 1.0, xc.a[:], ALU.add, ALU.mult, [sq.b, xc.b], [sq.b])
            for (col0, gt), tb in zip(((1536, gAB), (2048, gCD)), (tha, zca)):
                bg = mm_tm(col0, 512)
                A("act", AF.Tanh, tb.a[:], bg.a[:, :], [bg.b], [tb.b])
                STT("dve", gt.a[:], tb.a[:], 1.0, bg.a[:, :], ALU.add, ALU.mult, [tb.b, bg.b], [gt.b])
            A("act", AF.Tanh, th.a[:], sq.a[:], [sq.b], [th.b], scale=0.7978845608028654)
            STT("dve", ugf.a[:], th.a[:], 1.0, xc.a[:], ALU.add, ALU.mult, [th.b, xc.b], [ugf.b])
            for f_ in mid_hooks:
                f_()
            del mid_hooks[:]
            P.dma(cgd[rows, :], gCD.a[:, 0:256], [gCD.b], [DB(f"cg{seg}{t}")])
            if seg == "s" and (t < 2 or t >= nt - 2):
                bi, br = (0, t * 128) if t < 2 else (1, (t - (nt - 2)) * 128)
                P.dma(cgb_out[bi][br:br + 128, :], gCD.a[:, 0:256], [gCD.b], [DB(f"cgb_out{t}")])
            b0 = mm_tm(0, 512)
            CP("dve", as_.a[:], b0.a[:, 0:256], [b0.b], [as_.b])
            CP("dve", vs.a[:, :, 0:64], b0.a[:, 256:512].rearrange("p (h c) -> p h c", h=4), [b0.b], [vs.b])
            bt = bank()
            for c in range(4):
                TR(bfview(bt)[:, c, :], qkt.a[:, c * 128:(c + 1) * 128], ident_bf.a[:], [qkt.b, ident_bf.b], [bt.b])
            CP("act", qs.a[:], bfview(bt)[:, 0:2, :], [bt.b], [qs.b])
            CP("act", ks.a[:], bfview(bt)[:, 2:4, :], [bt.b], [ks.b])
            b2 = mm_tm(1024, 512)
            CP("act", pbuf.a[:], b2.a[:, :], [b2.b], [pbuf.b])
            P.dma(pdv[rows, :, :], pbuf.a[:].rearrange("p (g v) -> p g v", g=4), [pbuf.b], [DB(f"pd{seg}{t}")])
            def ln_tail(st=st, ugf=ugf, vnb=vnb, ugg=ugg, gAB=gAB):
                P.op("dve", lambda: dve.bn_stats(out=st.a[:, 4:10], in_=ugf.a[:, 256:512]), [ugf.b], [st.b])
                P.op("dve", lambda: dve.bn_aggr(out=st.a[:, 10:12], in_=st.a[:, 4:10]), [st.b], [st.b])
                TS("dve", st.a[:, 12:13], st.a[:, 11:12], 4e-5, None, ALU.add, None, [st.b], [st.b])
                rsq(13, 12, 14)
                TS("dve", ugf.a[:, 256:512], ugf.a[:, 256:512], st.a[:, 10:11], st.a[:, 13:14], ALU.subtract, ALU.mult, [ugf.b, st.b], [ugf.b])
                TT("dve", vnb.a[:], ugf.a[:, 256:512], sng.a[:], ALU.mult, [ugf.b, sng.b], [vnb.b])
                STT("dve", ugg.a[:], ugf.a[:, 0:256], 0.5, gAB.a[:, 256:512], ALU.mult, ALU.mult, [ugf.b, gAB.b], [ugg.b])
            pend_ln.append(ln_tail)
            def sgu_tail(vnb=vnb, ugg=ugg, yb=yb):
                bs = bank()
                for h in range(4):
                    MM(bs.a[:, h * 64:(h + 1) * 64], sguW.a[:, h, :], vnb.a[:, h * 64:(h + 1) * 64], h == 0, True, [sguW.b, vnb.b], [bs.b])
                for h in range(4):
                    STT("dve", yb.a[:, h * 64:(h + 1) * 64], bs.a[:, h * 64:(h + 1) * 64], sguB.a[:, h:h + 1], ugg.a[:, h * 64:(h + 1) * 64],
                        ALU.add, ALU.mult, [bs.b, sguB.b, ugg.b], [yb.b])
            pend_sgu.append(sgu_tail)

        def S2(t):
            par = cnt["s2"] % 2
            cnt["s2"] += 1
            r = rec[t]
            x, qs, gAB, gCD, yb = r["x"], r["q"], r["gAB"], r["gCD"], r["yb"]
            ycat, ycatT, nrm = L["ycat"][par], L["ycatT"][par], L["nrm"][par]
            ycA, ycD = ycbufs[par]
            rows = slice(128 * t, 128 * t + 128)
            dl = nal[t]
            n = len(dl)
            bo = bank()
            pinned.add(banks.index(bo))
            def scores(h):
                chunk, base = h // 2, (h % 2) * 64
                et = L["et"][h % 2]
                bA = bank()
                bB = bank() if n > 4 else None
                for i, (d, slot, is_int) in enumerate(dl):
                    bk = bA if i < 4 else bB
                    tgt = bk.a[:, (i % 4) * 128:(i % 4 + 1) * 128]
                    first = (i % 4 == 0)
                    if is_int:
                        MM(tgt, ident_bf.a[:], Bint.a[:, h * 5 + d + 2, :], first, False, [ident_bf.b, BintB[h * 5 + d + 2]], [bk.b])
                    else:
                        MM(tgt, ident_bf.a[:], Gt.a[:, h * 7 + d + 3, :], first, False, [ident_bf.b, GB[h * 7 + d + 3]], [bk.b])
                        MM(tgt, ident_bf.a[:], masks.a[:, slot, :], False, False, [ident_bf.b, masks.b], [bk.b])
                for i, (d, slot, is_int) in enumerate(dl):
                    bk = bA if i < 4 else bB
                    tgt = bk.a[:, (i % 4) * 128:(i % 4 + 1) * 128]
                    kk = L["k"][(t + d + 2) % 8]
                    MM(tgt, kk.a[base:base + 64, chunk, :], qs.a[base:base + 64, chunk, :], False, True, [kk.b, qs.b], [bk.b])
                na = min(n, 4)
                A("act", AF.Exp, et.a[:, 0:na, :], bA.a[:, 0:na * 128].rearrange("p (i q) -> p i q", i=na), [bA.b], [et.b], scale=0.125)
                if n > 4:
                    A("act", AF.Exp, et.a[:, 4:n, :], bB.a[:, 0:(n - 4) * 128].rearrange("p (i q) -> p i q", i=n - 4), [bB.b], [et.b], scale=0.125)

            def pv(h):
                et = L["et"][h % 2]
                for i, (d, slot, is_int) in enumerate(dl):
                    vv = L["v"][(t + d + 2) % 8]
                    MM(bo.a[:, h * 65:(h + 1) * 65], et.a[:, i, :], vv.a[:, h, :], i == 0, i == n - 1, [et.b, vv.b], [bo.b])
            def pool_mixer():
                bp = bank()
                for g in range(4):
                    ents = [e for e in pol[t] if e[0] == g]
                    for i, (_, which, slot) in enumerate(ents):
                        a_ = L["a"][(t + which + 2) % 8]
                        MM(bp.a[:, g * 64:(g + 1) * 64], bands.a[:, slot, :], a_.a[:, g * 64:(g + 1) * 64], i == 0, i == len(ents) - 1,
                           [bands.b, a_.b], [bp.b])
                TT("dve", ycat.a[:, 0:256], bp.a[:, 0:256], gAB.a[:, 0:256], ALU.mult, [bp.b, gAB.b], [ycA])
            scores(0)
            for h in range(4):
                if h + 1 < 4:
                    scores(h + 1)
                if h == 3:
                    pool_mixer()
                pv(h)
            pinned.discard(banks.index(bo))
            bov = bo.a[:, 0:260].rearrange("p (h c) -> p h c", h=4)
            P.op("dve", lambda: dve.reciprocal(out=nrm.a[:, 0:4], in_=bov[:, :, 64]), [bo.b], [nrm.b])
            for h in range(4):
                STT("dve", ycat.a[:, 256 + h * 64:256 + (h + 1) * 64], bo.a[:, h * 65:h * 65 + 64], nrm.a[:, h:h + 1], gCD.a[:, 256 + h * 64:256 + (h + 1) * 64],
                    ALU.mult, ALU.mult, [bo.b, nrm.b, gCD.b], [ycD])
            bt1, bt2 = bank(), bank()
            yTb = yTbufs[par]
            srcs = [(yb, 0, yb.b), (yb, 128, yb.b), (ycat, 0, ycA), (ycat, 128, ycA), (ycat, 256, ycD), (ycat, 384, ycD)]
            wperm = [2, 3, 0, 1, 4, 5]
            for i, (tb, o, bb) in enumerate(srcs):
                bt = bt1 if i < 3 else bt2
                TR(bfview(bt)[:, i % 3, :], tb.a[:, o:o + 128], ident_bf.a[:], [bb, ident_bf.b], [bt.b])
            CP("act", ycatT.a[:, 0:3, :], bfview(bt1)[:, 0:3, :], [bt1.b], [yTb[0]])
            CP("dve", ycatT.a[:, 3:6, :], bfview(bt2)[:, 0:3, :], [bt2.b], [yTb[1]])
            for nb in range(2):
                bw = bank()
                for i in range(6):
                    MM(bw.a[:, :], ycatT.a[:, i, :], wo_abd.a[:, wperm[i], nb * 512:(nb + 1) * 512], i == 0, i == 5, [yTb[i // 3], wo_abd.b], [bw.b])
                TT("dve", x.a[:, nb * 512:(nb + 1) * 512], bw.a[:, :], x.a[:, nb * 512:(nb + 1) * 512], ALU.add, [bw.b, x.b], [x.b])
            P.dma(xd[rows, :], x.a[:], [x.b], [DB(f"x{seg}{t}")])
            if seg == "s" and (t < 2 or t >= nt - 2):
                bi, br = (0, t * 128) if t < 2 else (1, (t - (nt - 2)) * 128)
                P.dma(xb_out[bi][br:br + 128, :], x.a[:], [x.b], [DB(f"xb_out{t}")])

        hTbufs = [[Buf(f"hT{p}_{c}") for c in range(8)] for p in range(2)] if "hTb" not in tile_layout_cache else tile_layout_cache["hTb"]
        tile_layout_cache["hTb"] = hTbufs
        yTbufs = tile_layout_cache.setdefault("yTb", [[Buf(f"yT{p}_{c}") for c in range(2)] for p in range(2)])
        ycbufs = tile_layout_cache.setdefault("ycb2", [[Buf(f"ycA{p}"), Buf(f"ycD{p}")] for p in range(2)])
        pend_sgu = []
        pend_ln = []
        mid_hooks = []
        LAG = 3
        own = [t for t in tiles if not halo(t)]
        done = 0
        nT = len(tiles)
        S1_load(tiles[0])
        if nT > 1:
            S1_load(tiles[1])
        if final:
            S1(tiles[0], "ac")
        else:
            S1(tiles[0], "a")
            S1(tiles[0], "t")
        for i, t in enumerate(tiles):
            if i + 2 < nT:
                S1_load(tiles[i + 2])
            if final:
                if i + 1 < nT:
                    S1(tiles[i + 1], "ac")
                S1(t, "an")
                continue
            if i + 1 < nT:
                S1(tiles[i + 1], "ac")
                mid_hooks.append(lambda tn=tiles[i + 1]: S1(tn, "an"))
            prev_sgu = list(pend_sgu)
            del pend_sgu[:]
            S1(t, "m")
            for f_ in mid_hooks:
                f_()
            del mid_hooks[:]
            for f_ in prev_sgu:
                f_()
            if i + 1 < nT:
                S1(tiles[i + 1], "t")
            for f_ in pend_ln:
                f_()
            del pend_ln[:]
            if final:
                continue
            while done < len(own) and own[done] + LAG <= t:
                S2(own[done])
                done += 1
        for f_ in pend_sgu:
            f_()
        del pend_sgu[:]
        if not final:
            while done < len(own):
                S2(own[done])
                done += 1

    def fft_phase(seg):
        arena_reset()
        N1 = cfg.N1p if seg == "p" else cfg.N1s
        Lfull = Lp if seg == "p" else Ls
        CH = min(64, 512 // (2 * N1))
        nbk = 64 // CH
        nbuf = 2 if seg == "p" else 1
        Pins = [carve(f"Pin{i}", [128, 128, 128], BF16) for i in range(nbuf)]
        Rrs = [carve(f"Rr{i}", [128, N1, 64], BF16) for i in range(nbuf)]
        Yp = [carve(f"Yp{i}", [128, CH, 2, N1], BF16) for i in range(3)]
        t1 = [carve(f"t1{i}", [128, CH, 2, N1], F32) for i in range(2)]
        t2 = [carve(f"t2{i}", [128, CH, 2, N1], F32) for i in range(2)]
        cs1, cs2 = cs_sb["cs1" + seg], cs_sb["cs2" + seg]
        twc, tws = tw_sb["twc" + seg], tw_sb["tws" + seg]
        twc_b = twc.a[:].unsqueeze(1).unsqueeze(1).to_broadcast([128, CH, 2, N1])
        tws_b = tws.a[:].unsqueeze(1).unsqueeze(1).to_broadcast([128, CH, 2, N1])
        scale = 1.0 / float(np.sqrt(64.0 * Lfull))
        ngroups = 4 if seg == "p" else 1
        it = 0
        for gi in range(ngroups):
            Pin, Rr = Pins[gi % nbuf], Rrs[gi % nbuf]
            if seg == "p":
                P.dma(Pin.a[0:N1, :, :].rearrange("p b v -> p (b v)"),
                      pp[gi * Lp:(gi + 1) * Lp, :].rearrange("(a b) v -> a (b v)", a=N1), [DB(f"pdp{t}") for t in range(NPT)], [Pin.b])
            else:
                for r in range(4):
                    P.op("sp", lambda r=r: sp.dma_start(out=Pin.a[r * NST:(r + 1) * NST, :, :].rearrange("p b v -> p (b v)"),
                                                        in_=dsl(p_all, 0, r * Lq, Lq).rearrange("(a b) v -> a (b v)", a=NST)),
                         [DB("p_all")], [Pin.b], kind="dma")
            add_eng = "pool" if seg == "s" else "dve"

            def stage2(c0, y):
                bZ = bank()
                MM(bZ.a[:, 0:CH * N1], c128.a[:], y.a[:, :, 0, :], True, False, [c128.b, y.b], [bZ.b])
                MM(bZ.a[:, 0:CH * N1], s128.a[:], y.a[:, :, 1, :], False, True, [s128.b, y.b], [bZ.b])
                A("act", AF.Identity, Rr.a[:, :, c0:c0 + CH].rearrange("p k c -> p c k"),
                  bZ.a[:, 0:CH * N1].rearrange("p (c k) -> p c k", c=CH), [bZ.b], [Rr.b], scale=scale)
            pend = None
            for bk in range(nbk):
                c0 = bk * CH
                y, a1, a2 = Yp[it % 3], t1[it % 2], t2[it % 2]
                it += 1
                bS = bank()
                for ci in range(CH):
                    c = c0 + ci
                    o = bS.a[:, ci * 2 * N1:(ci + 1) * 2 * N1]
                    MM(o, Pin.a[0:N1, :, c], cs1.a[0:N1, :], True, False, [Pin.b, cs1.b], [bS.b])
                    MM(o, Pin.a[0:N1, :, 64 + c], cs2.a[0:N1, :], False, True, [Pin.b, cs2.b], [bS.b])
                if pend is not None:
                    stage2(*pend)
                pv = bS.a[:, 0:CH * 2 * N1].rearrange("p (c r k) -> p c r k", c=CH, r=2)
                TT("dve", a1.a[:], pv, twc_b, ALU.mult, [bS.b, twc.b], [a1.b])
                TT("dve", a2.a[:], pv, tws_b, ALU.mult, [bS.b, tws.b], [a2.b])
                TT(add_eng, y.a[:, :, 0, :], a1.a[:, :, 0, :], a2.a[:, :, 1, :], ALU.add, [a1.b, a2.b], [y.b])
                TT(add_eng, y.a[:, :, 1, :], a1.a[:, :, 1, :], a2.a[:, :, 0, :], ALU.subtract, [a1.b, a2.b], [y.b])
                pend = (c0, y)
            stage2(*pend)
            if seg == "p":
                P.dma(res_p[gi * Lp:(gi + 1) * Lp, :].rearrange("(a b) c -> a (b c)", a=128), Rr.a[:].rearrange("p k c -> p (k c)"),
                      [Rr.b], [DB("res_p")])
            else:
                for h in range(NH):
                    kk = N1 // NH
                    P.dma(res_out[h].rearrange("(a b) c -> a (b c)", a=128), Rr.a[:, h * kk:(h + 1) * kk, :].rearrange("p k c -> p (k c)"),
                          [Rr.b], [DB(f"res_out{h}")])

    def dyn_gather():
        hw = 256 // NH
        for h in range(NH):
            ra = res_all[h].rearrange("(g t) c -> g t c", g=4)
            ro = res_own[h].rearrange("(g t) c -> g t c", g=4)
            rh = res_halo[h].rearrange("(g t) c -> g t c", g=4)
            P.op("sp", lambda ra=ra, ro=ro: sp.dma_start(out=ro, in_=ra[:, bass.ds(dynv.get(1), Lq // NH), :]), [DB(f"res_all{h}")], [DB("res_own")], kind="dma")
            P.op("sp", lambda ra=ra, rh=rh: sp.dma_start(out=rh[:, 0:hw, :], in_=ra[:, bass.ds(dynv.get(2), hw), :]), [DB(f"res_all{h}")], [DB("res_halo")], kind="dma")
            P.op("sp", lambda ra=ra, rh=rh: sp.dma_start(out=rh[:, hw:2 * hw, :], in_=ra[:, bass.ds(dynv.get(3), hw), :]), [DB(f"res_all{h}")], [DB("res_halo")], kind="dma")
        P.op("sp", lambda: sp.dma_start(out=xb_halo[0:256, :], in_=xb_all[1][bass.ds(dynv.get(4), 256), :]), [DB("xb_all1")], [DB("xb_halo")], kind="dma")
        P.op("sp", lambda: sp.dma_start(out=xb_halo[256:512, :], in_=xb_all[0][bass.ds(dynv.get(5), 256), :]), [DB("xb_all0")], [DB("xb_halo")], kind="dma")
        P.op("sp", lambda: sp.dma_start(out=cgb_halo[0:256, :], in_=cgb_all[1][bass.ds(dynv.get(4), 256), :]), [DB("cgb_all1")], [DB("cgb_halo")], kind="dma")
        P.op("sp", lambda: sp.dma_start(out=cgb_halo[256:512, :], in_=cgb_all[0][bass.ds(dynv.get(5), 256), :]), [DB("cgb_all0")], [DB("cgb_halo")], kind="dma")

    def allgather(src, dst, reads, writes):
        P.op("pool", lambda: pool.collective_compute("AllGather", ALU.bypass, replica_groups=GROUPS,
                                                     ins=[src.opt()], outs=[dst.opt()]), reads, writes, kind="cc")

    tile_layout_cache = {}
    _orig_tile_layout = tile_layout

    def tile_layout():
        if "L" not in tile_layout_cache:
            tile_layout_cache["L"] = _orig_tile_layout()
        return tile_layout_cache["L"]

    edge = [t for t in range(NST) if t < 2 or t >= NST - 2]
    class _Stop(Exception):
        pass

    def chk(tag):
        if cfg.stop == tag:
            raise _Stop()

    def whole_step():
        for l in range(DEPTH + 1):
            if l == 0:
                chk("prologue")
            if l < DEPTH:
                layer_prep(l)
            chk(f"prep{l}")
            if l == DEPTH:
                wout_prep(l, 1)
            chk(f"wout{l}")
            if l >= 1:
                dyn_gather()
            chk(f"gather{l}")
            tile_phase("s", l)
            chk(f"tiles_s{l}")
            if l < DEPTH:
                for i in range(2):
                    allgather(xb_out[i], xb_all[i], [DB(f"xb_out{t}") for t in edge], [DB(f"xb_all{i}")])
                    allgather(cgb_out[i], cgb_all[i], [DB(f"cgb_out{t}") for t in edge], [DB(f"cgb_all{i}")])
                for g in range(4):
                    allgather(p_out[g * Lq:(g + 1) * Lq, :], p_all[g * 4 * Lq:(g + 1) * 4 * Lq, :],
                              [DB(f"pds{t}") for t in range(NST)], [DB("p_all")])
            chk(f"ag{l}")
            wout_prep(l, 0)
            tile_phase("p", l)
            P.barrier()
            chk(f"tiles_p{l}")
            if l < DEPTH:
                fft_phase("s")
                chk(f"fft_s{l}")
                for h in range(NH):
                    allgather(res_out[h], res_all[h], [DB(f"res_out{h}")], [DB(f"res_all{h}")])
                P.barrier()
                fft_phase("p")
                P.barrier()
                chk(f"fft_p{l}")
    try:
        whole_step()
    except _Stop:
        pass
    if cfg.maxops is not None:
        del P.ops[cfg.maxops:]
    if cfg.dump:
        P.barrier()
        loc = dict(xpd=xpd, xsd=xsd, p_out=p_out, p_all=p_all, pp=pp, res_out=res_out[0], res_all=res_all[0], res_p=res_p,
                   cg_s=cg_s, cg_p=cg_p, xb_all0=xb_all[0], xb_all1=xb_all[1], ud=ud, gated=gated,
                   res_own=res_own[0], res_halo=res_halo[0], xb_halo=xb_halo, cgb_halo=cgb_halo)
        sbl = dict(W_eff=W_eff, gsT=gsT, shT=shT, Gt=Gt, Bint=Bint, wo_abd=wo_abd, wo_c=wo_c, sguW=sguW, sng=sng, sguB=sguB)
        for nm in cfg.dump:
            if nm in loc:
                src_ap = loc[nm]
                o = nc.dram_tensor("dbg_" + nm, list(src_ap.shape), src_ap.dtype, kind="ExternalOutput").ap()
                P.dma(o, src_ap, (), ())
            else:
                if nm not in sbl:
                    Lc = tile_layout_cache["L"]
                    base = nm.rstrip("0123456789")
                    tb = Lc[base][int(nm[len(base):])] if isinstance(Lc[base], list) else Lc[base]
                else:
                    tb = sbl[nm]
                shp = list(tb.a.shape)
                o = nc.dram_tensor("dbg_" + nm, shp, tb.a.dtype, kind="ExternalOutput").ap()
                P.dma(o, tb.a[:], (), ())
    P.finalize()
    return nc, P


_CACHE = {}


def make_in_maps(cfg, geo, inp):
    DEPTH, Lq, Ls = cfg.DEPTH, cfg.Lq, cfg.Ls
    f32 = np.float32
    k = dft_consts(cfg)
    w_in = np.asarray(inp["w_in"], f32)
    shared = {
        "w_ada": np.ascontiguousarray(inp["w_ada"], f32),
        "b_ada": np.ascontiguousarray(inp["b_ada"], f32),
        "norm_g_fm": np.ascontiguousarray(np.asarray(inp["norm_g"], f32).reshape(DEPTH, 8, 128).transpose(0, 2, 1)),
        "w_in": np.ascontiguousarray(w_in),
        "w_inT_ac": np.ascontiguousarray(np.concatenate([w_in[:, :, 0:256], w_in[:, :, 1280:1536]], axis=2).transpose(0, 2, 1)),
        "w_out": np.ascontiguousarray(inp["w_out"], f32),
        "pool_w": np.ascontiguousarray(inp["pool_w"], f32),
        "pool_scale": np.ascontiguousarray(inp["pool_scale"], f32),
        "sgu_norm_g": np.ascontiguousarray(inp["sgu_norm_g"], f32),
        "sgu_wT": np.ascontiguousarray(np.asarray(inp["sgu_w"], f32).transpose(0, 1, 3, 2)),
        "sgu_bT": np.ascontiguousarray(np.asarray(inp["sgu_b"], f32).transpose(0, 2, 1)),
        "fnet_w": np.ascontiguousarray(np.asarray(inp["fnet_w"], f32).reshape(DEPTH, 256, 64)),
        "na_rpbT": np.ascontiguousarray(np.asarray(inp["na_rpb"], f32).transpose(0, 3, 1, 2).reshape(DEPTH, 31, 60)),
        "final_norm_g": np.ascontiguousarray(np.asarray(inp["final_norm_g"], f32).reshape(1, D)),
    }
    for n in ("ident_bf", "ident_f", "c128", "s128", "cs1p", "cs2p", "cs1s", "cs2s", "twcp", "twsp", "twcs", "twss",
              "bdc64", "bds64", "ecol", "sel"):
        shared[n] = np.ascontiguousarray(k[n])
    xpr = np.asarray(inp["x_prompt"], f32)
    xsa = np.asarray(inp["x_sample"], f32)
    cpr = np.asarray(inp["c_prompt"], f32)
    csa = np.asarray(inp["c_sample"], f32)
    maps = []
    for i in range(8):
        s, j = i // 4, i % 4
        m = dict(shared)
        m["xp"] = np.ascontiguousarray(xpr[i])
        m["xs"] = np.ascontiguousarray(xsa[s, j * Lq:(j + 1) * Lq])
        xsh = np.zeros((512, D), f32)
        if j > 0:
            xsh[0:256] = xsa[s, j * Lq - 256:j * Lq]
        if j < 3:
            xsh[256:512] = xsa[s, (j + 1) * Lq:(j + 1) * Lq + 256]
        m["xsh"] = xsh
        cT = np.zeros((128, 16), f32)
        cT[:, 0::2] = cpr[i].reshape(8, 128).T
        cT[:, 1::2] = csa[s].reshape(8, 128).T
        m["cT"] = cT
        NH = 2 if Ls * 64 * 2 > (1 << 20) else 1
        m["tab"] = np.array([[j * 4 * Lq, j * Lq // NH, max(j * Lq - 256, 0) // NH, min((j + 1) * Lq, Ls - 256) // NH,
                              max(j - 1, 0) * 256, min(j + 1, 3) * 256, 0, 0]], np.int32)
        m["masks"] = np.ascontiguousarray(geo["masks"][i])
        m["bands"] = np.ascontiguousarray(geo["bands"][i])
        maps.append(m)
    return maps


def run(cfg, inp):
    key = (cfg.NPT, cfg.NST, cfg.DEPTH)
    if key not in _CACHE:
        geo = make_geometry(cfg)
        nc, P = build_program(cfg, geo)
        _CACHE[key] = (geo, nc, P)
    geo, nc, P = _CACHE[key]
    maps = make_in_maps(cfg, geo, inp)
    res = run_bass_kernel_spmd(nc, maps, core_ids=list(range(8)))
    yp = np.stack([np.asarray(res.results[i]["yp"], np.float32) for i in range(8)])
    ys = np.stack([np.concatenate([np.asarray(res.results[4 * s + j]["ys"], np.float32) for j in range(4)], 0) for s in range(2)])
    return yp, ys


def kernel(**inputs):
    cfg = Cfg(16, 32, 4)
    yp, ys = run(cfg, inputs)
    return (yp, ys)
```

```python
import sys
import numpy as np
import ml_dtypes
import concourse.bass as bass
import concourse.mybir as mybir
from concourse.bass_utils import run_bass_kernel_spmd

F32 = mybir.dt.float32
BF16 = mybir.dt.bfloat16
I32 = mybir.dt.int32
AF = mybir.ActivationFunctionType
ALU = mybir.AluOpType
NPBF = ml_dtypes.bfloat16

D = 1024
NEG = -240000.0
GELU_F = AF.Gelu_apprx_tanh


class Buf:
    __slots__ = ("name", "w", "r", "excl")

    def __init__(self, name, excl=False):
        self.name = name
        self.w = None
        self.r = []
        self.excl = excl


class Op:
    __slots__ = ("eng", "fn", "reads", "writes", "kind", "deps", "signal", "sem", "val", "clock", "idx", "inc", "tag")


class Prog:
    COMPUTE = ("pe", "act", "dve", "pool")

    def __init__(self, nc, n_dma_sems=28):
        self.nc = nc
        self.ops = []
        self.eng = {"pe": nc.tensor, "act": nc.scalar, "dve": nc.vector, "pool": nc.gpsimd, "sp": nc.sync}
        self.n_dma_sems = n_dma_sems

    def op(self, eng, fn, reads=(), writes=(), kind="c"):
        o = Op()
        o.eng, o.fn, o.kind = eng, fn, kind
        o.reads = [b for b in reads if b is not None and not b.excl]
        o.writes = [b for b in writes if b is not None] + [b for b in reads if b is not None and b.excl]
        o.idx = len(self.ops)
        f = sys._getframe(1)
        o.tag = (f.f_lineno, f.f_back.f_lineno if f.f_back else 0, f.f_back.f_back.f_lineno if f.f_back and f.f_back.f_back else 0)
        self.ops.append(o)
        return o

    def dma(self, out, in_, reads=(), writes=(), q="sp", **kw):
        e = self.eng[q]
        return self.op(q, lambda: e.dma_start(out=out, in_=in_, **kw), reads, writes, kind="dma")

    def barrier(self):
        self.op("sp", None, kind="bar")

    def finalize(self):
        nc = self.nc
        ops = self.ops
        for o in ops:
            o.clock = None
            o.signal = False
            o.sem = None
            if o.kind == "bar":
                o.deps = set()
                continue
            deps = set()
            for b in o.reads:
                if b.w is not None:
                    deps.add(b.w)
            for b in o.writes:
                if b.w is not None:
                    deps.add(b.w)
                deps.update(b.r)
            for b in o.reads:
                if o.kind in ("c", "reg"):
                    b.r = [r for r in b.r if not (ops[r].kind in ("c", "reg") and ops[r].eng == o.eng)]
                b.r.append(o.idx)
            for b in o.writes:
                b.w = o.idx
                b.r = []
            deps.discard(o.idx)
            if o.eng == "pe" and o.kind == "c":
                deps = {d for d in deps if not (ops[d].eng == "pe" and ops[d].kind == "c")}
            o.deps = deps
        dsems, dcnt, dlast, dnext = {}, {}, {}, {}
        for o in ops:
            if o.kind == "dma":
                q = o.eng
                if q not in dsems:
                    n = self.n_dma_sems if q == "sp" else 8
                    dsems[q] = [nc.alloc_semaphore(f"sem_d{q}{i}") for i in range(n)]
                    dcnt[q] = [0] * n
                    dlast[q] = [None] * n
                    dnext[q] = 0
                k = dnext[q]
                dnext[q] = (k + 1) % len(dsems[q])
                if dlast[q][k] is not None:
                    o.deps.add(dlast[q][k])
                dcnt[q][k] += 16
                dlast[q][k] = o.idx
                o.sem, o.val, o.inc = dsems[q][k], dcnt[q][k], 16
                o.signal = True
            elif o.kind == "cc":
                o.sem, o.val, o.inc = nc.alloc_semaphore(f"sem_cc{o.idx}"), 1, 1
                o.signal = True
        last_c = {}
        for o in ops:
            if o.kind == "bar":
                for idx in last_c.values():
                    ops[idx].signal = True
                continue
            for d in o.deps:
                ops[d].signal = True
            if o.kind == "c":
                last_c[o.eng] = o.idx
        for idx in last_c.values():
            ops[idx].signal = True
        esem = {e: nc.alloc_semaphore("sem_" + e) for e in self.COMPUTE}
        ecnt = {e: 0 for e in self.COMPUTE}
        for o in ops:
            if o.kind == "c" and o.signal:
                ecnt[o.eng] += 1
                o.sem, o.val, o.inc = esem[o.eng], ecnt[o.eng], 1
            elif o.kind == "reg":
                assert not o.signal, "register loads cannot be waited on"
        seen = {e: {} for e in self.eng}
        glob = {}
        ccglob = {}
        bar_clock = {}
        n_wait = 0
        for o in ops:
            if o.kind == "bar":
                bar_clock = dict(glob)
                continue
            e = self.eng[o.eng]
            sn = seen[o.eng]
            need = {}
            for d in o.deps:
                p = ops[d]
                k = id(p.sem)
                if sn.get(k, (None, 0))[1] < p.val and need.get(k, (None, 0))[1] < p.val:
                    need[k] = (p.sem, p.val)
            for k, sv in bar_clock.items():
                if sn.get(k, (None, 0))[1] < sv[1] and need.get(k, (None, 0))[1] < sv[1]:
                    need[k] = sv
            for d in o.deps:
                p = ops[d]
                if p.clock:
                    for k, sv in p.clock.items():
                        if k in need and need[k][1] <= sv[1] and k != id(p.sem):
                            pass
            for k, (s, v) in need.items():
                if sn.get(k, (None, 0))[1] < v:
                    e.wait_ge(s, v)
                    n_wait += 1
                    sn[k] = (s, v)
            for d in o.deps:
                p = ops[d]
                if p.clock:
                    for k, sv in p.clock.items():
                        if sn.get(k, (None, 0))[1] < sv[1]:
                            sn[k] = sv
            inst = o.fn()
            if o.signal:
                inst.then_inc(o.sem, o.inc)
                o.clock = dict(sn)
                o.clock[id(o.sem)] = (o.sem, o.val)
                tgt = ccglob if o.kind == "cc" else glob
                if tgt.get(id(o.sem), (None, 0))[1] < o.val:
                    tgt[id(o.sem)] = (o.sem, o.val)
        glob.update(ccglob)
        for k, (s, v) in glob.items():
            if seen["sp"].get(k, (None, 0))[1] < v:
                nc.sync.wait_ge(s, v)
        self.stats = dict(n_ops=len(ops), n_wait=n_wait, per_eng={e: sum(1 for o in ops if o.eng == e) for e in self.eng})


class Cfg:
    def __init__(self, NPT=16, NST=32, DEPTH=4, stop=None):
        self.NPT, self.NST, self.DEPTH = NPT, NST, DEPTH
        self.stop = stop
        self.dump = ()
        self.maxops = None
        self.Lp = 128 * NPT
        self.Lq = 128 * NST
        self.Ls = 4 * self.Lq
        self.N1p = NPT
        self.N1s = 4 * NST


def _na_geom(rows):
    kh = min(8, rows)
    r = np.arange(rows)
    rs = np.clip(r - kh // 2, 0, rows - kh)
    c = np.arange(64)
    cst = np.clip(c - 8, 0, 48)
    return kh, rs, cst


def na_mask_tile(rows, qt, kt):
    kh, rs, cst = _na_geom(rows)
    m = np.full((2, 64, 2, 64), NEG, np.float32)
    if kt < 0 or 2 * kt + 1 >= rows + 1 and 2 * kt >= rows:
        return m.reshape(128, 128)
    kc = np.arange(64)[:, None]
    qc = np.arange(64)[None, :]
    colok = (kc >= cst[qc]) & (kc < cst[qc] + 16)
    for kr in range(2):
        for qr in range(2):
            krow, qrow = 2 * kt + kr, 2 * qt + qr
            if krow < 0 or krow >= rows or qrow >= rows:
                continue
            if rs[qrow] <= krow < rs[qrow] + kh:
                m[kr, :, qr, :] = np.where(colok, 0.0, NEG)
    return m.reshape(128, 128)


def pool_band(L, gt, which, g):
    w = (2, 4, 8, 16)[g]
    out = np.zeros((128, 128), np.float32)
    for dst in range(128):
        t = 128 * gt + dst
        lo = min(max(t - w // 2, 0), L)
        hi = min(max(t - w // 2 + w, 0), L)
        cnt = hi - lo
        for s in range(lo, hi):
            sl = s - 128 * (gt + which)
            if 0 <= sl < 128:
                out[sl, dst] += 1.0 / cnt
        if which == 0:
            out[dst, dst] -= 1.0
    return out


def make_geometry(cfg):
    NPT, NST = cfg.NPT, cfg.NST
    rows_p, rows_s = 2 * NPT, 8 * NST
    geo = {}
    mask_tabs = [[] for _ in range(8)]
    slot_of = {}

    def add_shared(key, arr):
        if key not in slot_of:
            slot_of[key] = len(mask_tabs[0])
            for c in range(8):
                mask_tabs[c].append(arr)
        return slot_of[key]

    def add_percore(key, arrs):
        slot_of[key] = len(mask_tabs[0])
        for c in range(8):
            mask_tabs[c].append(arrs[c])
        return slot_of[key]

    interior = {d: na_mask_tile(64, 10, 10 + d) for d in range(-3, 4)}
    na_p = []
    for qt in range(NPT):
        lst = []
        for d in range(-3, 4):
            kt = qt + d
            if kt < 0 or kt >= NPT:
                continue
            m = na_mask_tile(rows_p, qt, kt)
            if (m == NEG).all():
                continue
            is_int = np.array_equal(m, interior[d])
            slot = add_shared(("int", d), m) if is_int else add_shared(("p", qt, d), m)
            lst.append((d, slot, is_int))
        na_p.append(lst)
    na_s = []
    for qt in range(NST):
        lst = []
        for d in range(-3, 4):
            kt = qt + d
            if kt < -2 or kt > NST + 1:
                continue
            ms = [na_mask_tile(rows_s, NST * (c % 4) + qt, NST * (c % 4) + kt) for c in range(8)]
            if all((m == NEG).all() for m in ms):
                continue
            if all(np.array_equal(m, interior[d]) for m in ms):
                slot = add_shared(("int", d), ms[0])
                lst.append((d, slot, True))
            else:
                slot = add_percore(("s", qt, d), ms)
                lst.append((d, slot, False))
        na_s.append(lst)
    geo["na_p"], geo["na_s"] = na_p, na_s
    geo["int_slots"] = {d: slot_of[("int", d)] for d in range(-3, 4) if ("int", d) in slot_of}
    geo["masks"] = [np.stack(t).astype(NPBF) for t in mask_tabs]
    band_tabs = [[] for _ in range(8)]
    bslot = {}

    def badd(key, arrs):
        if key not in bslot:
            bslot[key] = len(band_tabs[0])
            for c in range(8):
                band_tabs[c].append(arrs[c])
        return bslot[key]

    pool_p, pool_s = [], []
    for t in range(NPT):
        lst = []
        for g in range(4):
            for which in (-1, 0, 1):
                if t + which < 0 or t + which >= NPT:
                    continue
                m = pool_band(cfg.Lp, t, which, g)
                if not m.any():
                    continue
                mi = pool_band(128 * 64, 10, which, g)
                key = ("int", g, which) if np.array_equal(m, mi) else ("p", t, g, which)
                lst.append((g, which, badd(key, [m] * 8)))
        pool_p.append(lst)
    for t in range(NST):
        lst = []
        for g in range(4):
            for which in (-1, 0, 1):
                ms = [pool_band(cfg.Ls, NST * (c % 4) + t, which, g) for c in range(8)]
                if not any(m.any() for m in ms):
                    continue
                mi = pool_band(128 * 64, 10, which, g)
                if all(np.array_equal(m, mi) for m in ms):
                    key = ("int", g, which)
                else:
                    key = ("s", t, g, which)
                lst.append((g, which, badd(key, ms)))
        pool_s.append(lst)
    geo["pool_p"], geo["pool_s"] = pool_p, pool_s
    geo["bands"] = [np.stack(t).astype(NPBF) for t in band_tabs]
    return geo


def dft_consts(cfg):
    def cs(n):
        k = np.arange(n)
        ang = 2 * np.pi * np.outer(k, k) / n
        return np.cos(ang), np.sin(ang)
    c128, s128 = cs(128)
    out = {"c128": c128.astype(NPBF), "s128": s128.astype(NPBF)}
    for nm, n1, L in (("p", cfg.N1p, cfg.Lp), ("s", cfg.N1s, cfg.Ls)):
        c, s = cs(n1)
        out["cs1" + nm] = np.concatenate([c, -s], 1).astype(NPBF)
        out["cs2" + nm] = np.concatenate([s, c], 1).astype(NPBF)
        l2 = np.arange(128)[:, None]
        k1 = np.arange(n1)[None, :]
        ang = 2 * np.pi * l2 * k1 / L
        out["twc" + nm] = np.cos(ang).astype(np.float32)
        out["tws" + nm] = np.sin(ang).astype(np.float32)
    c64, s64 = cs(64)
    z = np.zeros((64, 64))
    out["bdc64"] = np.block([[c64, z], [z, c64]]).astype(np.float32)
    out["bds64"] = np.block([[-s64, z], [z, -s64]]).astype(np.float32)
    ecol = np.zeros((31, 64, 64), np.float32)
    for kc in range(64):
        for qc in range(64):
            b = kc - qc + 15
            if 0 <= b < 31:
                ecol[b, kc, qc] = 1.0
    out["ecol"] = ecol.reshape(31, 4096).astype(NPBF)
    out["ident_bf"] = np.eye(128, dtype=np.float32).astype(NPBF)
    out["ident_f"] = np.eye(128, dtype=np.float32)
    sel = np.zeros((2, 2, 128), np.float32)
    sel[0, 0, :] = 1.0
    sel[1, 1, :] = 1.0
    out["sel"] = sel.reshape(2, 256)
    return out


class TB:
    __slots__ = ("a", "b")

    def __init__(self, a, b):
        self.a, self.b = a, b


def build_program(cfg, geo, dbg=False):
    NPT, NST, DEPTH = cfg.NPT, cfg.NST, cfg.DEPTH
    Lp, Lq, Ls = cfg.Lp, cfg.Lq, cfg.Ls
    NM = geo["masks"][0].shape[0]
    NB = geo["bands"][0].shape[0]
    nc = bass.Bass("TRN2", target_bir_lowering=False)
    P = Prog(nc)
    act, dve, pool, pe, sp = nc.scalar, nc.vector, nc.gpsimd, nc.tensor, nc.sync

    def din(name, shape, dt=F32):
        return nc.dram_tensor(name, list(shape), dt, kind="ExternalInput").ap()

    def dscr(name, shape, dt=F32):
        return nc.dram_tensor(name, list(shape), dt).ap()

    xp_in = din("xp", [Lp, D])
    xs_in = din("xs", [Lq, D])
    xsh_in = din("xsh", [512, D])
    cT_in = din("cT", [128, 16])
    tab_in = din("tab", [1, 8], I32)
    w_ada = din("w_ada", [DEPTH, D, 3 * D])
    b_ada = din("b_ada", [DEPTH, 3 * D])
    ng_fm = din("norm_g_fm", [DEPTH, 128, 8])
    w_in = din("w_in", [DEPTH, D, 2816])
    w_inT = din("w_inT_ac", [DEPTH, 512, D])
    w_out = din("w_out", [DEPTH, D, D])
    pool_w = din("pool_w", [DEPTH, 4, 64, 64])
    pool_scale = din("pool_scale", [DEPTH, 256])
    sgu_ng = din("sgu_norm_g", [DEPTH, 256])
    sgu_wT = din("sgu_wT", [DEPTH, 4, 128, 128])
    sgu_bT = din("sgu_bT", [DEPTH, 128, 4])
    fnet_w = din("fnet_w", [DEPTH, 256, 64])
    rpbT = din("na_rpbT", [DEPTH, 31, 60])
    fin_g = din("final_norm_g", [1, D])
    k_ident_bf = din("ident_bf", [128, 128], BF16)
    k_ident_f = din("ident_f", [128, 128])
    k_c128 = din("c128", [128, 128], BF16)
    k_s128 = din("s128", [128, 128], BF16)
    k_cs = {n: din(n, [n1, 2 * n1], BF16) for n, n1 in (("cs1p", cfg.N1p), ("cs2p", cfg.N1p), ("cs1s", cfg.N1s), ("cs2s", cfg.N1s))}
    k_tw = {n: din(n, [128, n1]) for n, n1 in (("twcp", cfg.N1p), ("twsp", cfg.N1p), ("twcs", cfg.N1s), ("twss", cfg.N1s))}
    k_bdc = din("bdc64", [128, 128])
    k_bds = din("bds64", [128, 128])
    k_ecol = din("ecol", [31, 4096], BF16)
    k_sel = din("sel", [2, 256])
    k_masks = din("masks", [NM, 128, 128], BF16)
    k_bands = din("bands", [NB, 128, 128], BF16)

    yp_out = nc.dram_tensor("yp", [Lp, D], F32, kind="ExternalOutput").ap()
    ys_out = nc.dram_tensor("ys", [Lq, D], F32, kind="ExternalOutput").ap()

    xpd = dscr("xpd", [Lp, D])
    xsd = dscr("xsd", [Lq, D])
    p_out = dscr("p_out", [4 * Lq, 128], BF16)
    p_all = dscr("p_all", [16 * Lq, 128], BF16)
    pp = dscr("pp", [4 * Lp, 128], BF16)
    NH = 2 if Ls * 64 * 2 > (1 << 20) else 1
    assert NH == 1 or cfg.N1s == 128
    BLK = 128 // NH
    res_out = [dscr(f"res_out{h}", [Ls // NH, 64], BF16) for h in range(NH)]
    res_all = [dscr(f"res_all{h}", [4 * Ls // NH, 64], BF16) for h in range(NH)]
    res_p = dscr("res_p", [4 * Lp, 64], BF16)
    cg_s = dscr("cg_s", [Lq, 256], BF16)
    cg_p = dscr("cg_p", [Lp, 256], BF16)
    xb_out = [dscr(f"xb_out{i}", [256, D]) for i in range(2)]
    xb_all = [dscr(f"xb_all{i}", [1024, D]) for i in range(2)]
    cgb_out = [dscr(f"cgb_out{i}", [256, 256], BF16) for i in range(2)]
    cgb_all = [dscr(f"cgb_all{i}", [1024, 256], BF16) for i in range(2)]
    ud = dscr("ud", [60, 4096], BF16)
    res_own = [dscr(f"res_own{h}", [4 * Lq // NH, 64], BF16) for h in range(NH)]
    res_halo = [dscr(f"res_halo{h}", [4 * 512 // NH, 64], BF16) for h in range(NH)]
    xb_halo = dscr("xb_halo", [512, D])
    cgb_halo = dscr("cgb_halo", [512, 256], BF16)
    gated = dscr("gated", [DEPTH, 2, D])
    dbuf = {}

    def DB(name):
        if name not in dbuf:
            dbuf[name] = Buf(name)
        return dbuf[name]

    sb_used = [0]

    def salloc(name, shape, dt):
        t = nc.alloc_sbuf_tensor(name, list(shape), dt)
        n = int(np.prod(shape[1:])) * (4 if dt in (F32, I32) else 2)
        sb_used[0] += n
        return TB(t, Buf(name))

    W_eff = salloc("W_eff", [128, 8, 3072], BF16)
    wo_abd = salloc("wo_abd", [128, 6, 1024], BF16)
    wo_c = salloc("wo_c", [128, 2, 1024], BF16)
    wst = salloc("wst", [128, 2, 1024], F32)
    gbc = salloc("gbc", [128, 1024], F32)
    ident_bf = salloc("ident_bf_s", [128, 128], BF16)
    ident_f = salloc("ident_f_s", [128, 128], F32)
    c128 = salloc("c128_s", [128, 128], BF16)
    s128 = salloc("s128_s", [128, 128], BF16)
    cs_sb = {n: salloc(n + "_s", [n1, 2 * n1], BF16) for n, n1 in (("cs1p", cfg.N1p), ("cs2p", cfg.N1p), ("cs1s", cfg.N1s), ("cs2s", cfg.N1s))}
    tw_sb = {n: salloc(n + "_s", [128, n1], F32) for n, n1 in (("twcp", cfg.N1p), ("twsp", cfg.N1p), ("twcs", cfg.N1s), ("twss", cfg.N1s))}
    masks = salloc("masks_s", [128, NM, 128], BF16)
    bands = salloc("bands_s", [128, NB, 128], BF16)
    Gt = salloc("Gt", [128, 28, 128], BF16)
    Bint = salloc("Bint", [128, 20, 128], BF16)
    gsT = salloc("gsT", [128, DEPTH * 16], F32)
    shT = salloc("shT", [128, DEPTH * 16], F32)
    sguW = salloc("sguW", [128, 4, 128], BF16)
    sguB = salloc("sguB", [128, 4], F32)
    sng = salloc("sng", [128, 256], F32)
    gfin = salloc("gfin", [128, 1024], F32)
    tabs = salloc("tabs", [1, 8], I32)
    st_dummy = salloc("st_dummy", [1, 8], F32)
    BintB = [Buf(f"Bint{i}") for i in range(20)]
    WB = [[Buf(f"W{kc}_{g}") for g in range(12)] for kc in range(8)]

    def wb(kc, c0, n):
        return [WB[kc][g] for g in range(c0 // 256, (c0 + n - 1) // 256 + 1)]
    GB = [Buf(f"Gt{i}") for i in range(28)]
    ARENA = 48 * 1024
    arena = nc.alloc_sbuf_tensor("arena", [128, ARENA], BF16)
    sb_used[0] += ARENA * 2
    ar_off = [0]

    def carve(name, shape, dt, parts=128):
        n = int(np.prod(shape[1:]))
        ne = n * (2 if dt in (F32, I32) else 1)
        o = ar_off[0]
        o = (o + 1) // 2 * 2
        assert o + ne <= ARENA, f"arena overflow at {name}: {o + ne} > {ARENA}"
        v = arena[0:shape[0], o:o + ne]
        if dt in (F32, I32):
            v = v.bitcast(dt)
        if len(shape) == 3:
            v = v.rearrange("p (a b) -> p a b", b=shape[2])
        elif len(shape) == 4:
            v = v.rearrange("p (a b c) -> p a b c", b=shape[2], c=shape[3])
        ar_off[0] = o + ne
        return TB(v, Buf(name))

    def arena_reset():
        ar_off[0] = 0

    banks = [TB(nc.alloc_psum_tensor(f"bank{i}", [128, 512], F32), Buf(f"bank{i}", excl=True)) for i in range(8)]
    bank_rr = [0]
    pinned = set()

    def bank():
        while True:
            i = bank_rr[0] % 8
            bank_rr[0] += 1
            if i not in pinned:
                return banks[i]

    def A(eng, func, out, in_, reads, writes, **kw):
        e = P.eng[eng]
        return P.op(eng, lambda: e.activation(out=out, in_=in_, func=func, **kw), reads, writes)

    def CP(eng, out, in_, reads, writes):
        e = P.eng[eng]
        if eng == "act":
            return P.op(eng, lambda: e.activation(out=out, in_=in_, func=AF.Copy), reads, writes)
        return P.op(eng, lambda: e.tensor_copy(out=out, in_=in_), reads, writes)

    def TT(eng, out, in0, in1, op, reads, writes):
        e = P.eng[eng]
        return P.op(eng, lambda: e.tensor_tensor(out=out, in0=in0, in1=in1, op=op), reads, writes)

    def TS(eng, out, in0, s1, s2, op0, op1, reads, writes):
        e = P.eng[eng]
        if s2 is None:
            return P.op(eng, lambda: e.tensor_scalar(out=out, in0=in0, scalar1=s1, scalar2=None, op0=op0), reads, writes)
        return P.op(eng, lambda: e.tensor_scalar(out=out, in0=in0, scalar1=s1, scalar2=s2, op0=op0, op1=op1), reads, writes)

    def STT(eng, out, in0, scalar, in1, op0, op1, reads, writes):
        e = P.eng[eng]
        return P.op(eng, lambda: e.scalar_tensor_tensor(out=out, in0=in0, scalar=scalar, in1=in1, op0=op0, op1=op1), reads, writes)

    def MM(out, lhsT, rhs, start, stop, reads, writes):
        return P.op("pe", lambda: pe.matmul(out, lhsT, rhs, start=start, stop=stop), reads, writes)

    def TR(out, in_, ident, reads, writes):
        return P.op("pe", lambda: pe.transpose(out, in_, ident), reads, writes)

    def MS(eng, out, val, writes):
        e = P.eng[eng]
        return P.op(eng, lambda: e.memset(out, val), (), writes)

    for tb, src in ((ident_bf, k_ident_bf), (ident_f, k_ident_f), (c128, k_c128), (s128, k_s128)):
        P.dma(tb.a[:], src, (), [tb.b])
    for n in cs_sb:
        P.dma(cs_sb[n].a[:], k_cs[n], (), [cs_sb[n].b])
    for n in tw_sb:
        P.dma(tw_sb[n].a[:], k_tw[n], (), [tw_sb[n].b])
    P.dma(masks.a[:], k_masks.rearrange("m p q -> p m q"), (), [masks.b])
    P.dma(bands.a[:], k_bands.rearrange("m p q -> p m q"), (), [bands.b])
    P.dma(gfin.a[:], fin_g.to_broadcast([128, D]), (), [gfin.b])
    P.dma(tabs.a[:], tab_in, (), [tabs.b])
    regs = [sp.alloc_register(f"dyn{i}") for i in range(6)]
    dyn = []

    def _ld(i):
        return lambda: sp.reg_load(regs[i], tabs.a[0:1, i:i + 1])
    for i in range(6):
        P.op("sp", _ld(i), [tabs.b] if i == 0 else (), (), kind="reg")
    hi = [12 * Lq, 3 * Lq // NH, (Ls - 256) // NH, (Ls - 256) // NH, 768, 768]

    class _Dyn:
        def __init__(self):
            self.v = [None] * 6

        def get(self, i):
            if self.v[i] is None:
                self.v[i] = sp.snap(regs[i], min_val=0, max_val=hi[i])
            return self.v[i]
    dynv = _Dyn()

    GROUPS = [[0, 1, 2, 3], [4, 5, 6, 7]]

    def dsl(ap, i, add, n):
        return (ap[add:] if add else ap)[bass.ds(dynv.get(i), n)]

    arena_reset()
    wa32 = [carve(f"wa32_{i}", [128, 3072], F32) for i in range(4)]
    wa16 = [carve(f"wa16_{i}", [128, 3072], BF16) for i in range(2)]
    cT32 = carve("cT32", [128, 16], F32)
    sc16 = carve("sc16", [128, 16], BF16)
    modrow = carve("modrow", [2, 3072], F32)
    brow = carve("brow", [2, 3072], F32)
    ng_sb = carve("ng_sb", [128, 8], F32)
    tmp16 = carve("tmp16", [128, 16], F32)
    sel_sb = carve("sel_sb", [2, 256], F32)
    P.dma(cT32.a[:], cT_in, (), [cT32.b])
    A("act", AF.Silu, sc16.a[:], cT32.a[:], [cT32.b], [sc16.b])
    for l in range(DEPTH):
        mb = banks[0:6]
        for kc in range(8):
            w32, w16 = wa32[(l * 8 + kc) % 4], wa16[kc % 2]
            P.dma(w32.a[:], w_ada[l, kc * 128:(kc + 1) * 128, :], (), [w32.b])
            CP("pool", w16.a[:, 0:1024], w32.a[:, 0:1024], [w32.b], [w16.b])
            CP("dve", w16.a[:, 1024:2048], w32.a[:, 1024:2048], [w32.b], [w16.b])
            CP("act", w16.a[:, 2048:3072], w32.a[:, 2048:3072], [w32.b], [w16.b])
            for nb in range(6):
                MM(mb[nb].a[0:2, :], sc16.a[:, kc * 2:(kc + 1) * 2], w16.a[:, nb * 512:(nb + 1) * 512],
                   kc == 0, kc == 7, [sc16.b, w16.b], [mb[nb].b])
        P.dma(brow.a[:], b_ada[l:l + 1, :].to_broadcast([2, 3 * D]), (), [brow.b])
        for nb in range(6):
            TT("dve", modrow.a[0:2, nb * 512:(nb + 1) * 512], mb[nb].a[0:2, :], brow.a[0:2, nb * 512:(nb + 1) * 512],
               ALU.add, [mb[nb].b, brow.b], [modrow.b])
        P.dma(gated[l], modrow.a[0:2, 2048:3072], [modrow.b], [DB("gated")])
        bt = banks[6]
        for ch in range(16):
            TR(bt.a[:, ch * 2:(ch + 1) * 2], modrow.a[0:2, ch * 128:(ch + 1) * 128], ident_f.a[0:2, 0:2],
               [modrow.b, ident_f.b], [bt.b])
        P.dma(ng_sb.a[:], ng_fm[l], (), [ng_sb.b])
        CP("dve", shT.a[:, l * 16:(l + 1) * 16], bt.a[:, 0:16], [bt.b], [shT.b])
        TS("dve", tmp16.a[:], bt.a[:, 16:32], 1.0, None, ALU.add, None, [bt.b], [tmp16.b])
        TT("dve", gsT.a[:, l * 16:(l + 1) * 16].rearrange("p (k s) -> p k s", s=2),
           tmp16.a[:].rearrange("p (k s) -> p k s", s=2),
           ng_sb.a[:].unsqueeze(2).to_broadcast([128, 8, 2]), ALU.mult, [tmp16.b, ng_sb.b], [gsT.b])
    P.barrier()

    def layer_prep(l):
        arena_reset()
        wi32 = [carve(f"wi32_{i}", [128, 2816], F32) for i in range(4)]
        wT32 = carve("wT32", [128, 4, 1024], F32)
        BDa = carve("BDa", [128, 2, 256], F32)
        BDr = carve("BDr", [128, 2, 256], F32)
        BDi = carve("BDi", [128, 2, 256], F32)
        psb = carve("psb", [128, 256], F32)
        fws = carve("fws", [128, 2, 64], F32)
        bdc = carve("bdc", [128, 128], F32)
        bds = carve("bds", [128, 128], F32)
        ecol = carve("ecol", [31, 4096], BF16)
        rpb32 = carve("rpb32", [31, 60], F32)
        rpb16 = carve("rpb16", [31, 60], BF16)
        U_sb = carve("U_sb", [60, 4096], BF16)
        sgw32 = carve("sgw32", [128, 4, 128], F32)
        P.dma(sgw32.a[:], sgu_wT[l].rearrange("h q p -> q h p"), (), [sgw32.b])
        CP("dve", sguW.a[:], sgw32.a[:], [sgw32.b], [sguW.b])
        P.dma(sguB.a[:], sgu_bT[l], (), [sguB.b])
        P.dma(sng.a[:], sgu_ng[l:l + 1, :].to_broadcast([128, 256]), (), [sng.b])
        MS("dve", BDa.a[:], 0.0, [BDa.b])
        MS("dve", BDr.a[:], 0.0, [BDr.b])
        MS("dve", BDi.a[:], 0.0, [BDi.b])
        for g in range(4):
            P.dma(BDa.a[(g % 2) * 64:(g % 2 + 1) * 64, g // 2, g * 64:(g + 1) * 64], pool_w[l, g], (), [BDa.b])
        P.dma(psb.a[:], pool_scale[l:l + 1, :].to_broadcast([128, 256]), (), [psb.b])
        for cc in range(2):
            TT("dve", BDa.a[:, cc, :], BDa.a[:, cc, :], psb.a[:], ALU.mult, [BDa.b, psb.b], [BDa.b])
        P.dma(fws.a[:], fnet_w[l].rearrange("(c p) d -> p c d", c=2), (), [fws.b])
        P.dma(bdc.a[:], k_bdc, (), [bdc.b])
        P.dma(bds.a[:], k_bds, (), [bds.b])
        for cc in range(2):
            bx = bank()
            MM(bx.a[:, 0:64], bdc.a[:], fws.a[:, cc, :], True, True, [bdc.b, fws.b], [bx.b])
            MM(bx.a[:, 64:128], bds.a[:], fws.a[:, cc, :], True, True, [bds.b, fws.b], [bx.b])
            for gl in range(2):
                g = 2 * cc + gl
                CP("act", BDr.a[gl * 64:(gl + 1) * 64, cc, g * 64:(g + 1) * 64], bx.a[gl * 64:(gl + 1) * 64, 0:64], [bx.b], [BDr.b])
                CP("act", BDi.a[gl * 64:(gl + 1) * 64, cc, g * 64:(g + 1) * 64], bx.a[gl * 64:(gl + 1) * 64, 64:128], [bx.b], [BDi.b])
        P.dma(wT32.a[:], w_inT[l].rearrange("(j p) r -> p j r", j=4), (), [wT32.b])
        copies = [(256, 512, 1536, 0.5), (512, 1024, 512, 1.0), (1024, 1280, 1792, 0.5), (1536, 1792, 2048, 0.5),
                  (1792, 2304, 2560, 1.0), (2304, 2560, 256, 1.0), (2560, 2816, 2304, 0.5)]
        engs = ["dve", "dve", "dve", "act", "dve", "dve", "act"]
        for kc in range(8):
            wi = wi32[kc % 4]
            P.dma(wi.a[:], w_in[l, kc * 128:(kc + 1) * 128, :], (), [wi.b])
            for (s0, s1, d0, sc), en in zip(copies, engs):
                if en == "act":
                    A("act", AF.Identity, W_eff.a[:, kc, d0:d0 + (s1 - s0)], wi.a[:, s0:s1], [wi.b], wb(kc, d0, s1 - s0), scale=sc)
                else:
                    TS(en, W_eff.a[:, kc, d0:d0 + (s1 - s0)], wi.a[:, s0:s1], sc, 0.0, ALU.mult, ALU.add, [wi.b], wb(kc, d0, s1 - s0))
            ba = bank()
            for cc in range(2):
                MM(ba.a[:, 0:256], wT32.a[:, cc, kc * 128:(kc + 1) * 128], BDa.a[:, cc, :], cc == 0, cc == 1,
                   [wT32.b, BDa.b], [ba.b])
            CP("act", W_eff.a[:, kc, 0:256], ba.a[:, 0:256], [ba.b], wb(kc, 0, 256))
            bc = bank()
            for cc in range(2):
                MM(bc.a[:, 0:256], wT32.a[:, 2 + cc, kc * 128:(kc + 1) * 128], BDr.a[:, cc, :], cc == 0, cc == 1,
                   [wT32.b, BDr.b], [bc.b])
            for cc in range(2):
                MM(bc.a[:, 256:512], wT32.a[:, 2 + cc, kc * 128:(kc + 1) * 128], BDi.a[:, cc, :], cc == 0, cc == 1,
                   [wT32.b, BDi.b], [bc.b])
            CP("dve", W_eff.a[:, kc, 1024:1536].rearrange("p (g r d) -> p g r d", g=4, r=2),
               bc.a[:, :].rearrange("p (r g d) -> p g r d", r=2, g=4), [bc.b], wb(kc, 1024, 512))
        P.dma(ecol.a[:], k_ecol, (), [ecol.b])
        P.dma(rpb32.a[:], rpbT[l], (), [rpb32.b])
        CP("dve", rpb16.a[:], rpb32.a[:], [rpb32.b], [rpb16.b])
        for ch in range(8):
            bu = bank()
            MM(bu.a[0:60, :], rpb16.a[:], ecol.a[:, ch * 512:(ch + 1) * 512], True, True, [rpb16.b, ecol.b], [bu.b])
            A("act", AF.Identity, U_sb.a[:, ch * 512:(ch + 1) * 512], bu.a[0:60, :], [bu.b], [U_sb.b], scale=8.0)
        P.dma(ud, U_sb.a[:], [U_sb.b], [DB("ud")])
        MS("dve", Gt.a[:], 0.0, GB)
        udv = ud.rearrange("r (k q) -> r k q", k=64)
        GB4 = {(i, kr, qr): Buf(f"Gt{i}_{kr}{qr}") for i in range(28) for kr in range(2) for qr in range(2)}
        P.op("dve", lambda: dve.memset(st_dummy.a[:], 0.0), GB, [GB4[k] for k in GB4])
        for h in range(4):
            for kr in range(2):
                for qr in range(2):
                    a0 = 2 * (-3) + kr - qr + 7
                    assert 0 <= a0 and a0 + 12 <= 14
                    P.dma(Gt.a[kr * 64:(kr + 1) * 64, h * 7:h * 7 + 7, qr * 64:(qr + 1) * 64],
                          udv[h * 15 + a0:h * 15 + a0 + 13:2].rearrange("d k q -> k d q"),
                          [DB("ud")], [GB4[(h * 7 + dd, kr, qr)] for dd in range(7)])
        for h in range(4):
            for d in range(-2, 3):
                if d in geo["int_slots"]:
                    gi_ = h * 7 + d + 3
                    TT("dve", Bint.a[:, h * 5 + d + 2, :], Gt.a[:, gi_, :],
                       masks.a[:, geo["int_slots"][d], :], ALU.add, [GB4[(gi_, kr, qr)] for kr in range(2) for qr in range(2)] + [masks.b], [BintB[h * 5 + d + 2]])
        gb1 = carve("gb1", [128, 1024], F32)
        gb2 = carve("gb2", [128, 1024], F32)
        wout_prep(l, 1, slots=[(wi32[i].a[:, 0:1024], wi32[i].b) for i in range(4)], gates=[gb1, gb2])
        for i in range(28):
            P.op("dve", lambda: dve.memset(st_dummy.a[:], 0.0), [GB4[(i, kr, qr)] for kr in range(2) for qr in range(2)], [GB[i]])
        P.barrier()

    def wout_prep(l, s, slots=None, gates=None):
        if slots is None:
            slots = [(wst.a[:, i, :], wstb[i]) for i in range(2)]
        if gates is None:
            gates = [gbc, gbc]
        k = 0
        if l >= 1:
            g_ = gates[0]
            P.dma(g_.a[:], gated[l - 1, s:s + 1, :].to_broadcast([128, D]), [DB("gated")], [g_.b])
            for i, kc in enumerate((4, 5)):
                sa, sb_ = slots[k % len(slots)]
                P.dma(sa, w_out[l - 1, kc * 128:(kc + 1) * 128, :], (), [sb_])
                TT("dve", wo_c.a[:, i, :], sa, g_.a[:], ALU.mult, [sb_, g_.b], [wo_c.b])
                k += 1
        if l < DEPTH:
            g_ = gates[1]
            P.dma(g_.a[:], gated[l, s:s + 1, :].to_broadcast([128, D]), [DB("gated")], [g_.b])
            for i, kc in enumerate((0, 1, 2, 3, 6, 7)):
                sa, sb_ = slots[k % len(slots)]
                P.dma(sa, w_out[l, kc * 128:(kc + 1) * 128, :], (), [sb_])
                TT("dve", wo_abd.a[:, i, :], sa, g_.a[:], ALU.mult, [sb_, g_.b], [wo_abd.b])
                k += 1
    wstb = [Buf("wst0"), Buf("wst1")]
    def tile_layout():
        arena_reset()
        L = {}
        L["x"] = [carve(f"x{i}", [128, 1024], F32) for i in range(6)]
        L["k"] = [carve(f"k{i}", [128, 2, 128], BF16) for i in range(8)]
        L["v"] = [carve(f"v{i}", [128, 4, 65], BF16) for i in range(8)]
        L["a"] = [carve(f"a{i}", [128, 256], BF16) for i in range(8)]
        L["q"] = [carve(f"q{i}", [128, 2, 128], BF16) for i in range(5)]
        L["yb"] = [carve(f"yb{i}", [128, 256], BF16) for i in range(5)]
        L["gAB"] = [carve(f"gAB{i}", [128, 512], BF16) for i in range(5)]
        L["gCD"] = [carve(f"gCD{i}", [128, 512], BF16) for i in range(5)]
        for nm in ("sq", "xc", "th", "tha", "zca"):
            L[nm] = carve(nm, [128, 512], F32)
        for nm, shp, dt in (("fo", [128, 256], BF16), ("cg", [128, 256], BF16), ("ycb", [128, 256], BF16),
                            ("ycT", [128, 2, 128], BF16), ("xhat", [128, 1024], BF16), ("hT", [128, 8, 128], BF16),
                            ("ugf", [128, 512], F32), ("pbuf", [128, 512], BF16), ("qkt", [128, 512], BF16),
                            ("st", [128, 16], F32), ("vnb", [128, 256], BF16), ("ugg", [128, 256], F32),
                            ("ycat", [128, 512], BF16), ("ycatT", [128, 6, 128], BF16), ("et", [128, 6, 128], BF16),
                            ("nrm", [128, 8], F32)):
            L[nm] = [carve(f"{nm}{i}", shp, dt) for i in range(2)]
        L["junk"] = carve("junk", [128, 1024], BF16)
        return L

    def bfview(bk):
        return bk.a[:].bitcast(BF16).rearrange("p (c t) -> p c t", t=128)

    def tile_phase(seg, l):
        L = tile_layout()
        s = 0 if seg == "p" else 1
        nt = NPT if seg == "p" else NST
        tiles = list(range(nt)) if seg == "p" else list(range(-2, nt + 2))
        final = (l == DEPTH)
        for v in L["v"]:
            MS("dve", v.a[:], 1.0, [v.b])
        xsrc0 = xp_in if seg == "p" else xs_in
        xd = xpd if seg == "p" else xsd
        cgd = cg_p if seg == "p" else cg_s
        pd = pp if seg == "p" else p_out
        yout = yp_out if seg == "p" else ys_out
        Lseg = Lp if seg == "p" else Lq
        resv_p = res_p.rearrange("(g t) c -> t g c", g=4)
        resv_s = [r_.rearrange("(g t) c -> t g c", g=4) for r_ in res_own]
        resv_h = [r_.rearrange("(g t) c -> t g c", g=4) for r_ in res_halo]
        pdv = pd.rearrange("(g t) v -> t g v", g=4)
        nal = geo["na_p"] if seg == "p" else geo["na_s"]
        pol = geo["pool_p"] if seg == "p" else geo["pool_s"]
        rec = {}
        cnt = {"x": 0, "ld": 0, "s2": 0, "q": 0}

        def halo(t):
            return seg == "s" and (t < 0 or t >= nt)

        def S1_load(t):
            par = cnt["ld"] % 2
            cnt["ld"] += 1
            hl = halo(t)
            x = L["x"][cnt["x"] % 6]
            cnt["x"] += 1
            r = {"x": x, "par": par}
            rec[t] = r
            rows = slice(128 * t, 128 * t + 128)
            xbuf = DB(f"x{seg}{t}")
            if l == 0:
                if hl:
                    hi_ = (t + 2) if t < 0 else (2 + t - nt)
                    P.dma(x.a[:], xsh_in[hi_ * 128:(hi_ + 1) * 128, :], (), [x.b])
                else:
                    P.dma(x.a[:], xsrc0[rows, :], (), [x.b])
            elif hl:
                hi_ = (t + 2) if t < 0 else (2 + t - nt)
                P.dma(x.a[:], xb_halo[hi_ * 128:(hi_ + 1) * 128, :], [DB("xb_halo")], [x.b])
            else:
                P.dma(x.a[:], xd[rows, :], [xbuf], [x.b])
            if l >= 1:
                fo, cg = L["fo"][par], L["cg"][par]
                fov = fo.a[:].rearrange("p (g c) -> p g c", g=4)
                if seg == "p":
                    P.dma(fov, resv_p[rows, :, :], [DB("res_p")], [fo.b])
                    P.dma(cg.a[:], cgd[rows, :], [DB(f"cg{seg}{t}")], [cg.b])
                else:
                    if hl:
                        hi_ = (t + 2) if t < 0 else (2 + t - nt)
                        hr = slice(hi_ * 128, (hi_ + 1) * 128)
                        P.dma(cg.a[:], cgb_halo[hr, :], [DB("cgb_halo")], [cg.b])
                        for h in range(NH):
                            P.dma(fov[h * BLK:(h + 1) * BLK], resv_h[h][hi_ * BLK:(hi_ + 1) * BLK, :, :], [DB("res_halo")], [fo.b])
                    else:
                        P.dma(cg.a[:], cgd[rows, :], [DB(f"cg{seg}{t}")], [cg.b])
                        for h in range(NH):
                            P.dma(fov[h * BLK:(h + 1) * BLK], resv_s[h][t * BLK:(t + 1) * BLK, :, :], [DB("res_own")], [fo.b])

        def S1(t, part):
            r = rec[t]
            par, x = r["par"], r["x"]
            hl = halo(t)
            rows = slice(128 * t, 128 * t + 128)
            st = L["st"][par]
            junk = L["junk"]
            xhat, hT = L["xhat"][par], L["hT"][par]
            hTb = hTbufs[par]
            if part in ("a", "ac", "an"):
                S1a(t, r, par, x, hl, rows, st, junk, xhat, {"a": "both", "ac": "c", "an": "n"}[part])
            elif part == "t" and not final:
                bte, bto = bank(), bank()
                for c in range(8):
                    bt = bte if c % 2 == 0 else bto
                    TR(bfview(bt)[:, c // 2, :], xhat.a[:, c * 128:(c + 1) * 128], ident_bf.a[:], [xhat.b, ident_bf.b], [bt.b])
                for c in range(8):
                    col = l * 16 + c * 2 + s
                    if c % 2 == 0:
                        TS("dve", hT.a[:, c, :], bfview(bte)[:, c // 2, :], gsT.a[:, col:col + 1], shT.a[:, col:col + 1], ALU.mult, ALU.add,
                           [bte.b, gsT.b, shT.b], [hTb[c]])
                    else:
                        A("act", AF.Identity, hT.a[:, c, :], bfview(bto)[:, c // 2, :], [bto.b, gsT.b, shT.b], [hTb[c]],
                          scale=gsT.a[:, col:col + 1], bias=shT.a[:, col:col + 1])
            elif part == "m" and not final:
                S1m(t, r, par, x, hl, rows, st, hT, hTb)

        def S1a(t, r, par, x, hl, rows, st, junk, xhat, which):
            if l >= 1 and which in ("both", "c"):
                fo, cg, ycb, ycT = L["fo"][par], L["cg"][par], L["ycb"][par], L["ycT"][par]
                TT("dve", ycb.a[:], fo.a[:], cg.a[:], ALU.mult, [fo.b, cg.b], [ycb.b])
                bt = bank()
                btv = bfview(bt)
                for c in range(2):
                    TR(btv[:, c, :], ycb.a[:, c * 128:(c + 1) * 128], ident_bf.a[:], [ycb.b, ident_bf.b], [bt.b])
                CP("act", ycT.a[:], btv[:, 0:2, :], [bt.b], [ycT.b])
                for nb in range(2):
                    bw = bank()
                    for c in range(2):
                        MM(bw.a[:, :], ycT.a[:, c, :], wo_c.a[:, c, nb * 512:(nb + 1) * 512], c == 0, c == 1, [ycT.b, wo_c.b], [bw.b])
                    TT("dve", x.a[:, nb * 512:(nb + 1) * 512], bw.a[:, :], x.a[:, nb * 512:(nb + 1) * 512], ALU.add, [bw.b, x.b], [x.b])
            if which == "c":
                return

            def rsq(yc, vc, tc):
                y, v, tm = st.a[:, yc:yc + 1], st.a[:, vc:vc + 1], st.a[:, tc:tc + 1]
                TS("dve", y.bitcast(I32), v.bitcast(I32), 1, None, ALU.arith_shift_right, None, [st.b], [st.b])
                TS("dve", y.bitcast(I32), y.bitcast(I32), -1, 0x5f3759df, ALU.mult, ALU.add, [st.b], [st.b])
                for _ in range(2):
                    STT("dve", tm, y, v, y, ALU.mult, ALU.mult, [st.b], [st.b])
                    STT("dve", tm, tm, -0.5, y, ALU.mult, ALU.mult, [st.b], [st.b])
                    STT("dve", y, y, 1.5, tm, ALU.mult, ALU.add, [st.b], [st.b])

            A("act", AF.Square, junk.a[:], x.a[:], [x.b], [junk.b, st.b], accum_out=st.a[:, 0:1])
            TS("dve", st.a[:, 1:2], st.a[:, 0:1], 1.0 / D, 1e-6, ALU.mult, ALU.add, [st.b], [st.b])
            rsq(2, 1, 3)
            if final:
                if hl:
                    return
                A("act", AF.Identity, x.a[:], x.a[:], [x.b, st.b], [x.b], scale=st.a[:, 2:3])
                TT("pool", x.a[:], x.a[:], gfin.a[:], ALU.mult, [x.b, gfin.b], [x.b])
                P.dma(yout[rows, :], x.a[:], [x.b], [DB(f"y{seg}{t}")])
                return
            A("act", AF.Identity, xhat.a[:], x.a[:], [x.b, st.b], [xhat.b], scale=st.a[:, 2:3])

        def S1m(t, r, par, x, hl, rows, st, hT, hTb):
            def rsq(yc, vc, tc):
                y, v, tm = st.a[:, yc:yc + 1], st.a[:, vc:vc + 1], st.a[:, tc:tc + 1]
                TS("dve", y.bitcast(I32), v.bitcast(I32), 1, None, ALU.arith_shift_right, None, [st.b], [st.b])
                TS("dve", y.bitcast(I32), y.bitcast(I32), -1, 0x5f3759df, ALU.mult, ALU.add, [st.b], [st.b])
                for _ in range(2):
                    STT("dve", tm, y, v, y, ALU.mult, ALU.mult, [st.b], [st.b])
                    STT("dve", tm, tm, -0.5, y, ALU.mult, ALU.mult, [st.b], [st.b])
                    STT("dve", y, y, 1.5, tm, ALU.mult, ALU.add, [st.b], [st.b])
            ks = L["k"][(t + 2) % 8]
            vs = L["v"][(t + 2) % 8]
            as_ = L["a"][(t + 2) % 8]

            def mm_tm(col0, n):
                bk = bank()
                for kc in range(8):
                    MM(bk.a[:, 0:n], hT.a[:, kc, :], W_eff.a[:, kc, col0:col0 + n], kc == 0, kc == 7, [hTb[kc]] + wb(kc, col0, n), [bk.b])
                return bk

            def mm_fm(ms):
                bk = bank()
                for m in ms:
                    for kc in range(8):
                        MM(bk.a[:, m * 128:(m + 1) * 128], W_eff.a[:, kc, 2560 + m * 128:2560 + (m + 1) * 128], hT.a[:, kc, :],
                           kc == 0, kc == 7, [hTb[kc]] + wb(kc, 2560 + m * 128, 128), [bk.b])
                return bk
            if hl:
                b0 = mm_tm(0, 512)
                CP("dve", as_.a[:], b0.a[:, 0:256], [b0.b], [as_.b])
                CP("dve", vs.a[:, :, 0:64], b0.a[:, 256:512].rearrange("p (h c) -> p h c", h=4), [b0.b], [vs.b])
                qkt = L["qkt"][par]
                bq = mm_tm(2816, 256)
                CP("act", qkt.a[:, 256:512], bq.a[:, 0:256], [bq.b], [qkt.b])
                bt = bank()
                for c in range(2):
                    TR(bfview(bt)[:, c, :], qkt.a[:, 256 + c * 128:256 + (c + 1) * 128], ident_bf.a[:], [qkt.b, ident_bf.b], [bt.b])
                CP("act", ks.a[:], bfview(bt)[:, 0:2, :], [bt.b], [ks.b])
                return
            qi = cnt["q"] % 5
            cnt["q"] += 1
            qs, gAB, gCD, yb = L["q"][qi], L["gAB"][qi], L["gCD"][qi], L["yb"][qi]
            r.update(q=qs, gAB=gAB, gCD=gCD, yb=yb)
            ugf, pbuf, vnb, ugg = L["ugf"][par], L["pbuf"][par], L["vnb"][par], L["ugg"][par]
            sq, xc, th, tha, zca = L["sq"], L["xc"], L["th"], L["tha"], L["zca"]
            qkt = L["qkt"][par]
            bq = mm_tm(2560, 512)
            CP("act", qkt.a[:], bq.a[:, :], [bq.b], [qkt.b])
            b1 = mm_tm(512, 512)
            A("act", AF.Square, sq.a[:], b1.a[:, :], [b1.b], [sq.b], scale=0.21145921592587737)
            A("act", AF.Copy, xc.a[:], b1.a[:, :], [b1.b], [xc.b])
            STT("dve", sq.a[:], sq.a[:], 1.0, xc.a[:], ALU.add, ALU.mult, [sq.b, xc.b], [sq.b])
            for (col0, gt), tb in zip(((1536, gAB), (2048, gCD)), (tha, zca)):
                bg = mm_tm(col0, 512)
                A("act", AF.Tanh, tb.a[:], bg.a[:, :], [bg.b], [tb.b])
                STT("dve", gt.a[:], tb.a[:], 1.0, bg.a[:, :], ALU.add, ALU.mult, [tb.b, bg.b], [gt.b])
            A("act", AF.Tanh, th.a[:], sq.a[:], [sq.b], [th.b], scale=0.7978845608028654)
            STT("dve", ugf.a[:], th.a[:], 1.0, xc.a[:], ALU.add, ALU.mult, [th.b, xc.b], [ugf.b])
            P.dma(cgd[rows, :], gCD.a[:, 0:256], [gCD.b], [DB(f"cg{seg}{t}")])
            if seg == "s" and (t < 2 or t >= nt - 2):
                bi, br = (0, t * 128) if t < 2 else (1, (t - (nt - 2)) * 128)
                P.dma(cgb_out[bi][br:br + 128, :], gCD.a[:, 0:256], [gCD.b], [DB(f"cgb_out{t}")])
            b0 = mm_tm(0, 512)
            CP("dve", as_.a[:], b0.a[:, 0:256], [b0.b], [as_.b])
            CP("dve", vs.a[:, :, 0:64], b0.a[:, 256:512].rearrange("p (h c) -> p h c", h=4), [b0.b], [vs.b])
            bt = bank()
            for c in range(4):
                TR(bfview(bt)[:, c, :], qkt.a[:, c * 128:(c + 1) * 128], ident_bf.a[:], [qkt.b, ident_bf.b], [bt.b])
            CP("act", qs.a[:], bfview(bt)[:, 0:2, :], [bt.b], [qs.b])
            CP("act", ks.a[:], bfview(bt)[:, 2:4, :], [bt.b], [ks.b])
            b2 = mm_tm(1024, 512)
            CP("act", pbuf.a[:], b2.a[:, :], [b2.b], [pbuf.b])
            P.dma(pdv[rows, :, :], pbuf.a[:].rearrange("p (g v) -> p g v", g=4), [pbuf.b], [DB(f"pd{seg}{t}")])
            def ln_tail(st=st, ugf=ugf, vnb=vnb, ugg=ugg, gAB=gAB):
                P.op("dve", lambda: dve.bn_stats(out=st.a[:, 4:10], in_=ugf.a[:, 256:512]), [ugf.b], [st.b])
                P.op("dve", lambda: dve.bn_aggr(out=st.a[:, 10:12], in_=st.a[:, 4:10]), [st.b], [st.b])
                TS("dve", st.a[:, 12:13], st.a[:, 11:12], 4e-5, None, ALU.add, None, [st.b], [st.b])
                rsq(13, 12, 14)
                TS("dve", ugf.a[:, 256:512], ugf.a[:, 256:512], st.a[:, 10:11], st.a[:, 13:14], ALU.subtract, ALU.mult, [ugf.b, st.b], [ugf.b])
                TT("dve", vnb.a[:], ugf.a[:, 256:512], sng.a[:], ALU.mult, [ugf.b, sng.b], [vnb.b])
                STT("dve", ugg.a[:], ugf.a[:, 0:256], 0.5, gAB.a[:, 256:512], ALU.mult, ALU.mult, [ugf.b, gAB.b], [ugg.b])
            pend_ln.append(ln_tail)
            def sgu_tail(vnb=vnb, ugg=ugg, yb=yb):
                bs = bank()
                for h in range(4):
                    MM(bs.a[:, h * 64:(h + 1) * 64], sguW.a[:, h, :], vnb.a[:, h * 64:(h + 1) * 64], h == 0, True, [sguW.b, vnb.b], [bs.b])
                for h in range(4):
                    STT("dve", yb.a[:, h * 64:(h + 1) * 64], bs.a[:, h * 64:(h + 1) * 64], sguB.a[:, h:h + 1], ugg.a[:, h * 64:(h + 1) * 64],
                        ALU.add, ALU.mult, [bs.b, sguB.b, ugg.b], [yb.b])
            pend_sgu.append(sgu_tail)

        def S2(t):
            par = cnt["s2"] % 2
            cnt["s2"] += 1
            r = rec[t]
            x, qs, gAB, gCD, yb = r["x"], r["q"], r["gAB"], r["gCD"], r["yb"]
            ycat, ycatT, nrm = L["ycat"][par], L["ycatT"][par], L["nrm"][par]
            ycA, ycD = ycbufs[par]
            ycDs = ycdbufs[par]
            rows = slice(128 * t, 128 * t + 128)
            dl = nal[t]
            n = len(dl)
            bos = [bank(), bank()]
            for b_ in bos:
                pinned.add(banks.index(b_))
            def scores(h):
                chunk, base = h // 2, (h % 2) * 64
                et = L["et"][h % 2]
                bA = bank()
                bB = bank() if n > 4 else None
                for i, (d, slot, is_int) in enumerate(dl):
                    bk = bA if i < 4 else bB
                    tgt = bk.a[:, (i % 4) * 128:(i % 4 + 1) * 128]
                    first = (i % 4 == 0)
                    if is_int:
                        MM(tgt, ident_bf.a[:], Bint.a[:, h * 5 + d + 2, :], first, False, [ident_bf.b, BintB[h * 5 + d + 2]], [bk.b])
                    else:
                        MM(tgt, ident_bf.a[:], Gt.a[:, h * 7 + d + 3, :], first, False, [ident_bf.b, GB[h * 7 + d + 3]], [bk.b])
                        MM(tgt, ident_bf.a[:], masks.a[:, slot, :], False, False, [ident_bf.b, masks.b], [bk.b])
                for i, (d, slot, is_int) in enumerate(dl):
                    bk = bA if i < 4 else bB
                    tgt = bk.a[:, (i % 4) * 128:(i % 4 + 1) * 128]
                    kk = L["k"][(t + d + 2) % 8]
                    MM(tgt, kk.a[base:base + 64, chunk, :], qs.a[base:base + 64, chunk, :], False, True, [kk.b, qs.b], [bk.b])
                na = min(n, 4)
                A("act", AF.Exp, et.a[:, 0:na, :], bA.a[:, 0:na * 128].rearrange("p (i q) -> p i q", i=na), [bA.b], [et.b], scale=0.125)
                if n > 4:
                    A("act", AF.Exp, et.a[:, 4:n, :], bB.a[:, 0:(n - 4) * 128].rearrange("p (i q) -> p i q", i=n - 4), [bB.b], [et.b], scale=0.125)

            def pv(h):
                et = L["et"][h % 2]
                for i, (d, slot, is_int) in enumerate(dl):
                    vv = L["v"][(t + d + 2) % 8]
                    bo = bos[h // 2]
                    hh = h % 2
                    MM(bo.a[:, hh * 65:(hh + 1) * 65], et.a[:, i, :], vv.a[:, h, :], i == 0 and hh == 0, i == n - 1, [et.b, vv.b], [bo.b])
            def pool_mixer():
                bp = bank()
                for g in range(4):
                    ents = [e for e in pol[t] if e[0] == g]
                    for i, (_, which, slot) in enumerate(ents):
                        a_ = L["a"][(t + which + 2) % 8]
                        MM(bp.a[:, g * 64:(g + 1) * 64], bands.a[:, slot, :], a_.a[:, g * 64:(g + 1) * 64], i == 0, i == len(ents) - 1,
                           [bands.b, a_.b], [bp.b])
                TT("dve", ycat.a[:, 0:256], bp.a[:, 0:256], gAB.a[:, 0:256], ALU.mult, [bp.b, gAB.b], [ycA])
            def normalise(pr):
                bo = bos[pr]
                bov = bo.a[:, 0:130].rearrange("p (h c) -> p h c", h=2)
                nb_ = nrmbufs[par][pr]
                P.op("dve", lambda: dve.reciprocal(out=nrm.a[:, 2 * pr:2 * pr + 2], in_=bov[:, :, 64]), [bo.b], [nb_])
                for hh in range(2):
                    h = 2 * pr + hh
                    STT("dve", ycat.a[:, 256 + h * 64:256 + (h + 1) * 64], bo.a[:, hh * 65:hh * 65 + 64], nrm.a[:, h:h + 1],
                        gCD.a[:, 256 + h * 64:256 + (h + 1) * 64], ALU.mult, ALU.mult, [bo.b, nb_, gCD.b], [ycDs[pr]])
            scores(0)
            for h in range(4):
                if h + 1 < 4:
                    scores(h + 1)
                if h == 3:
                    pool_mixer()
                pv(h)
                if h % 2 == 1:
                    normalise(h // 2)
            for b_ in bos:
                pinned.discard(banks.index(b_))
            bt1, bt2 = bank(), bank()
            yTb = yTbufs[par]
            srcs = [(yb, 0, yb.b), (yb, 128, yb.b), (ycat, 0, ycA), (ycat, 128, ycA), (ycat, 256, ycDs[0]), (ycat, 384, ycDs[1])]
            wperm = [2, 3, 0, 1, 4, 5]
            for i, (tb, o, bb) in enumerate(srcs):
                bt = bt1 if i < 3 else bt2
                TR(bfview(bt)[:, i % 3, :], tb.a[:, o:o + 128], ident_bf.a[:], [bb, ident_bf.b], [bt.b])
            CP("act", ycatT.a[:, 0:3, :], bfview(bt1)[:, 0:3, :], [bt1.b], [yTb[0]])
            CP("dve", ycatT.a[:, 3:6, :], bfview(bt2)[:, 0:3, :], [bt2.b], [yTb[1]])
            for nb in range(2):
                bw = bank()
                for i in range(6):
                    MM(bw.a[:, :], ycatT.a[:, i, :], wo_abd.a[:, wperm[i], nb * 512:(nb + 1) * 512], i == 0, i == 5, [yTb[i // 3], wo_abd.b], [bw.b])
                TT("dve", x.a[:, nb * 512:(nb + 1) * 512], bw.a[:, :], x.a[:, nb * 512:(nb + 1) * 512], ALU.add, [bw.b, x.b], [x.b])
            P.dma(xd[rows, :], x.a[:], [x.b], [DB(f"x{seg}{t}")])
            if seg == "s" and (t < 2 or t >= nt - 2):
                bi, br = (0, t * 128) if t < 2 else (1, (t - (nt - 2)) * 128)
                P.dma(xb_out[bi][br:br + 128, :], x.a[:], [x.b], [DB(f"xb_out{t}")])

        hTbufs = [[Buf(f"hT{p}_{c}") for c in range(8)] for p in range(2)] if "hTb" not in tile_layout_cache else tile_layout_cache["hTb"]
        tile_layout_cache["hTb"] = hTbufs
        yTbufs = tile_layout_cache.setdefault("yTb", [[Buf(f"yT{p}_{c}") for c in range(2)] for p in range(2)])
        ycbufs = tile_layout_cache.setdefault("ycb2", [[Buf(f"ycA{p}"), Buf(f"ycD{p}")] for p in range(2)])
        ycdbufs = tile_layout_cache.setdefault("ycd2", [[Buf(f"ycD{p}_{i}") for i in range(2)] for p in range(2)])
        nrmbufs = tile_layout_cache.setdefault("nrm2", [[Buf(f"nrm{p}_{i}") for i in range(2)] for p in range(2)])
        pend_sgu = []
        pend_ln = []
        LAG = 3
        own = [t for t in tiles if not halo(t)]
        done = 0
        nT = len(tiles)
        S1_load(tiles[0])
        if nT > 1:
            S1_load(tiles[1])
        if final:
            S1(tiles[0], "ac")
        else:
            S1(tiles[0], "a")
            S1(tiles[0], "t")
        for i, t in enumerate(tiles):
            if i + 2 < nT:
                S1_load(tiles[i + 2])
            if final:
                if i + 1 < nT:
                    S1(tiles[i + 1], "ac")
                S1(t, "an")
                continue
            if i + 1 < nT:
                S1(tiles[i + 1], "a")
            prev_sgu = list(pend_sgu)
            del pend_sgu[:]
            S1(t, "m")
            for f_ in prev_sgu:
                f_()
            if i + 1 < nT:
                S1(tiles[i + 1], "t")
            for f_ in pend_ln:
                f_()
            del pend_ln[:]
            if final:
                continue
            while done < len(own) and own[done] + LAG <= t:
                S2(own[done])
                done += 1
        for f_ in pend_sgu:
            f_()
        del pend_sgu[:]
        if not final:
            while done < len(own):
                S2(own[done])
                done += 1

    def fft_phase(seg):
        arena_reset()
        N1 = cfg.N1p if seg == "p" else cfg.N1s
        Lfull = Lp if seg == "p" else Ls
        CH = min(64, 512 // (2 * N1))
        nbk = 64 // CH
        nbuf = 2 if seg == "p" else 1
        Pins = [carve(f"Pin{i}", [128, 128, 128], BF16) for i in range(nbuf)]
        Rrs = [carve(f"Rr{i}", [128, N1, 64], BF16) for i in range(nbuf)]
        Yp = [carve(f"Yp{i}", [128, CH, 2, N1], BF16) for i in range(3)]
        t1 = [carve(f"t1{i}", [128, CH, 2, N1], F32) for i in range(2)]
        t2 = [carve(f"t2{i}", [128, CH, 2, N1], F32) for i in range(2)]
        cs1, cs2 = cs_sb["cs1" + seg], cs_sb["cs2" + seg]
        twc, tws = tw_sb["twc" + seg], tw_sb["tws" + seg]
        twc_b = twc.a[:].unsqueeze(1).unsqueeze(1).to_broadcast([128, CH, 2, N1])
        tws_b = tws.a[:].unsqueeze(1).unsqueeze(1).to_broadcast([128, CH, 2, N1])
        scale = 1.0 / float(np.sqrt(64.0 * Lfull))
        ngroups = 4 if seg == "p" else 1
        it = 0
        for gi in range(ngroups):
            Pin, Rr = Pins[gi % nbuf], Rrs[gi % nbuf]
            if seg == "p":
                P.dma(Pin.a[0:N1, :, :].rearrange("p b v -> p (b v)"),
                      pp[gi * Lp:(gi + 1) * Lp, :].rearrange("(a b) v -> a (b v)", a=N1), [DB(f"pdp{t}") for t in range(NPT)], [Pin.b])
            else:
                for r in range(4):
                    P.op("sp", lambda r=r: sp.dma_start(out=Pin.a[r * NST:(r + 1) * NST, :, :].rearrange("p b v -> p (b v)"),
                                                        in_=dsl(p_all, 0, r * Lq, Lq).rearrange("(a b) v -> a (b v)", a=NST)),
                         [DB("p_all")], [Pin.b], kind="dma")
            add_eng = "pool" if seg == "s" else "dve"

            def stage2(c0, y):
                bZ = bank()
                MM(bZ.a[:, 0:CH * N1], c128.a[:], y.a[:, :, 0, :], True, False, [c128.b, y.b], [bZ.b])
                MM(bZ.a[:, 0:CH * N1], s128.a[:], y.a[:, :, 1, :], False, True, [s128.b, y.b], [bZ.b])
                A("act", AF.Identity, Rr.a[:, :, c0:c0 + CH].rearrange("p k c -> p c k"),
                  bZ.a[:, 0:CH * N1].rearrange("p (c k) -> p c k", c=CH), [bZ.b], [Rr.b], scale=scale)
            pend = None
            for bk in range(nbk):
                c0 = bk * CH
                y, a1, a2 = Yp[it % 3], t1[it % 2], t2[it % 2]
                it += 1
                bS = bank()
                for ci in range(CH):
                    c = c0 + ci
                    o = bS.a[:, ci * 2 * N1:(ci + 1) * 2 * N1]
                    MM(o, Pin.a[0:N1, :, c], cs1.a[0:N1, :], True, False, [Pin.b, cs1.b], [bS.b])
                    MM(o, Pin.a[0:N1, :, 64 + c], cs2.a[0:N1, :], False, True, [Pin.b, cs2.b], [bS.b])
                if pend is not None:
                    stage2(*pend)
                pv = bS.a[:, 0:CH * 2 * N1].rearrange("p (c r k) -> p c r k", c=CH, r=2)
                TT("dve", a1.a[:], pv, twc_b, ALU.mult, [bS.b, twc.b], [a1.b])
                TT("dve", a2.a[:], pv, tws_b, ALU.mult, [bS.b, tws.b], [a2.b])
                TT(add_eng, y.a[:, :, 0, :], a1.a[:, :, 0, :], a2.a[:, :, 1, :], ALU.add, [a1.b, a2.b], [y.b])
                TT(add_eng, y.a[:, :, 1, :], a1.a[:, :, 1, :], a2.a[:, :, 0, :], ALU.subtract, [a1.b, a2.b], [y.b])
                pend = (c0, y)
            stage2(*pend)
            if seg == "p":
                P.dma(res_p[gi * Lp:(gi + 1) * Lp, :].rearrange("(a b) c -> a (b c)", a=128), Rr.a[:].rearrange("p k c -> p (k c)"),
                      [Rr.b], [DB("res_p")])
            else:
                for h in range(NH):
                    kk = N1 // NH
                    P.dma(res_out[h].rearrange("(a b) c -> a (b c)", a=128), Rr.a[:, h * kk:(h + 1) * kk, :].rearrange("p k c -> p (k c)"),
                          [Rr.b], [DB(f"res_out{h}")])

    def dyn_gather():
        hw = 256 // NH
        for h in range(NH):
            ra = res_all[h].rearrange("(g t) c -> g t c", g=4)
            ro = res_own[h].rearrange("(g t) c -> g t c", g=4)
            rh = res_halo[h].rearrange("(g t) c -> g t c", g=4)
            P.op("sp", lambda ra=ra, ro=ro: sp.dma_start(out=ro, in_=ra[:, bass.ds(dynv.get(1), Lq // NH), :]), [DB(f"res_all{h}")], [DB("res_own")], kind="dma")
            P.op("sp", lambda ra=ra, rh=rh: sp.dma_start(out=rh[:, 0:hw, :], in_=ra[:, bass.ds(dynv.get(2), hw), :]), [DB(f"res_all{h}")], [DB("res_halo")], kind="dma")
            P.op("sp", lambda ra=ra, rh=rh: sp.dma_start(out=rh[:, hw:2 * hw, :], in_=ra[:, bass.ds(dynv.get(3), hw), :]), [DB(f"res_all{h}")], [DB("res_halo")], kind="dma")
        P.op("sp", lambda: sp.dma_start(out=xb_halo[0:256, :], in_=xb_all[1][bass.ds(dynv.get(4), 256), :]), [DB("xb_all1")], [DB("xb_halo")], kind="dma")
        P.op("sp", lambda: sp.dma_start(out=xb_halo[256:512, :], in_=xb_all[0][bass.ds(dynv.get(5), 256), :]), [DB("xb_all0")], [DB("xb_halo")], kind="dma")
        P.op("sp", lambda: sp.dma_start(out=cgb_halo[0:256, :], in_=cgb_all[1][bass.ds(dynv.get(4), 256), :]), [DB("cgb_all1")], [DB("cgb_halo")], kind="dma")
        P.op("sp", lambda: sp.dma_start(out=cgb_halo[256:512, :], in_=cgb_all[0][bass.ds(dynv.get(5), 256), :]), [DB("cgb_all0")], [DB("cgb_halo")], kind="dma")

    def allgather(src, dst, reads, writes):
        P.op("pool", lambda: pool.collective_compute("AllGather", ALU.bypass, replica_groups=GROUPS,
                                                     ins=[src.opt()], outs=[dst.opt()]), reads, writes, kind="cc")

    tile_layout_cache = {}
    _orig_tile_layout = tile_layout

    def tile_layout():
        if "L" not in tile_layout_cache:
            tile_layout_cache["L"] = _orig_tile_layout()
        return tile_layout_cache["L"]

    edge = [t for t in range(NST) if t < 2 or t >= NST - 2]
    class _Stop(Exception):
        pass

    def chk(tag):
        if cfg.stop == tag:
            raise _Stop()

    def whole_step():
        for l in range(DEPTH + 1):
            if l == 0:
                chk("prologue")
            if l < DEPTH:
                layer_prep(l)
            chk(f"prep{l}")
            if l == DEPTH:
                wout_prep(l, 1)
            chk(f"wout{l}")
            if l >= 1:
                dyn_gather()
            chk(f"gather{l}")
            tile_phase("s", l)
            chk(f"tiles_s{l}")
            if l < DEPTH:
                for i in range(2):
                    allgather(xb_out[i], xb_all[i], [DB(f"xb_out{t}") for t in edge], [DB(f"xb_all{i}")])
                    allgather(cgb_out[i], cgb_all[i], [DB(f"cgb_out{t}") for t in edge], [DB(f"cgb_all{i}")])
                for g in range(4):
                    allgather(p_out[g * Lq:(g + 1) * Lq, :], p_all[g * 4 * Lq:(g + 1) * 4 * Lq, :],
                              [DB(f"pds{t}") for t in range(NST)], [DB("p_all")])
            chk(f"ag{l}")
            wout_prep(l, 0)
            tile_phase("p", l)
            P.barrier()
            chk(f"tiles_p{l}")
            if l < DEPTH:
                fft_phase("s")
                chk(f"fft_s{l}")
                for h in range(NH):
                    allgather(res_out[h], res_all[h], [DB(f"res_out{h}")], [DB(f"res_all{h}")])
                P.barrier()
                fft_phase("p")
                P.barrier()
                chk(f"fft_p{l}")
    try:
        whole_step()
    except _Stop:
        pass
    if cfg.maxops is not None:
        del P.ops[cfg.maxops:]
    if cfg.dump:
        P.barrier()
        loc = dict(xpd=xpd, xsd=xsd, p_out=p_out, p_all=p_all, pp=pp, res_out=res_out[0], res_all=res_all[0], res_p=res_p,
                   cg_s=cg_s, cg_p=cg_p, xb_all0=xb_all[0], xb_all1=xb_all[1], ud=ud, gated=gated,
                   res_own=res_own[0], res_halo=res_halo[0], xb_halo=xb_halo, cgb_halo=cgb_halo)
        sbl = dict(W_eff=W_eff, gsT=gsT, shT=shT, Gt=Gt, Bint=Bint, wo_abd=wo_abd, wo_c=wo_c, sguW=sguW, sng=sng, sguB=sguB)
        for nm in cfg.dump:
            if nm in loc:
                src_ap = loc[nm]
                o = nc.dram_tensor("dbg_" + nm, list(src_ap.shape), src_ap.dtype, kind="ExternalOutput").ap()
                P.dma(o, src_ap, (), ())
            else:
                if nm not in sbl:
                    Lc = tile_layout_cache["L"]
                    base = nm.rstrip("0123456789")
                    tb = Lc[base][int(nm[len(base):])] if isinstance(Lc[base], list) else Lc[base]
                else:
                    tb = sbl[nm]
                shp = list(tb.a.shape)
                o = nc.dram_tensor("dbg_" + nm, shp, tb.a.dtype, kind="ExternalOutput").ap()
                P.dma(o, tb.a[:], (), ())
    P.finalize()
    return nc, P


_CACHE = {}


def make_in_maps(cfg, geo, inp):
    DEPTH, Lq, Ls = cfg.DEPTH, cfg.Lq, cfg.Ls
    f32 = np.float32
    k = dft_consts(cfg)
    w_in = np.asarray(inp["w_in"], f32)
    shared = {
        "w_ada": np.ascontiguousarray(inp["w_ada"], f32),
        "b_ada": np.ascontiguousarray(inp["b_ada"], f32),
        "norm_g_fm": np.ascontiguousarray(np.asarray(inp["norm_g"], f32).reshape(DEPTH, 8, 128).transpose(0, 2, 1)),
        "w_in": np.ascontiguousarray(w_in),
        "w_inT_ac": np.ascontiguousarray(np.concatenate([w_in[:, :, 0:256], w_in[:, :, 1280:1536]], axis=2).transpose(0, 2, 1)),
        "w_out": np.ascontiguousarray(inp["w_out"], f32),
        "pool_w": np.ascontiguousarray(inp["pool_w"], f32),
        "pool_scale": np.ascontiguousarray(inp["pool_scale"], f32),
        "sgu_norm_g": np.ascontiguousarray(inp["sgu_norm_g"], f32),
        "sgu_wT": np.ascontiguousarray(np.asarray(inp["sgu_w"], f32).transpose(0, 1, 3, 2)),
        "sgu_bT": np.ascontiguousarray(np.asarray(inp["sgu_b"], f32).transpose(0, 2, 1)),
        "fnet_w": np.ascontiguousarray(np.asarray(inp["fnet_w"], f32).reshape(DEPTH, 256, 64)),
        "na_rpbT": np.ascontiguousarray(np.asarray(inp["na_rpb"], f32).transpose(0, 3, 1, 2).reshape(DEPTH, 31, 60)),
        "final_norm_g": np.ascontiguousarray(np.asarray(inp["final_norm_g"], f32).reshape(1, D)),
    }
    for n in ("ident_bf", "ident_f", "c128", "s128", "cs1p", "cs2p", "cs1s", "cs2s", "twcp", "twsp", "twcs", "twss",
              "bdc64", "bds64", "ecol", "sel"):
        shared[n] = np.ascontiguousarray(k[n])
    xpr = np.asarray(inp["x_prompt"], f32)
    xsa = np.asarray(inp["x_sample"], f32)
    cpr = np.asarray(inp["c_prompt"], f32)
    csa = np.asarray(inp["c_sample"], f32)
    maps = []
    for i in range(8):
        s, j = i // 4, i % 4
        m = dict(shared)
        m["xp"] = np.ascontiguousarray(xpr[i])
        m["xs"] = np.ascontiguousarray(xsa[s, j * Lq:(j + 1) * Lq])
        xsh = np.zeros((512, D), f32)
        if j > 0:
            xsh[0:256] = xsa[s, j * Lq - 256:j * Lq]
        if j < 3:
            xsh[256:512] = xsa[s, (j + 1) * Lq:(j + 1) * Lq + 256]
        m["xsh"] = xsh
        cT = np.zeros((128, 16), f32)
        cT[:, 0::2] = cpr[i].reshape(8, 128).T
        cT[:, 1::2] = csa[s].reshape(8, 128).T
        m["cT"] = cT
        NH = 2 if Ls * 64 * 2 > (1 << 20) else 1
        m["tab"] = np.array([[j * 4 * Lq, j * Lq // NH, max(j * Lq - 256, 0) // NH, min((j + 1) * Lq, Ls - 256) // NH,
                              max(j - 1, 0) * 256, min(j + 1, 3) * 256, 0, 0]], np.int32)
        m["masks"] = np.ascontiguousarray(geo["masks"][i])
        m["bands"] = np.ascontiguousarray(geo["bands"][i])
        maps.append(m)
    return maps


def run(cfg, inp):
    key = (cfg.NPT, cfg.NST, cfg.DEPTH)
    if key not in _CACHE:
        geo = make_geometry(cfg)
        nc, P = build_program(cfg, geo)
        _CACHE[key] = (geo, nc, P)
    geo, nc, P = _CACHE[key]
    maps = make_in_maps(cfg, geo, inp)
    res = run_bass_kernel_spmd(nc, maps, core_ids=list(range(8)))
    yp = np.stack([np.asarray(res.results[i]["yp"], np.float32) for i in range(8)])
    ys = np.stack([np.concatenate([np.asarray(res.results[4 * s + j]["ys"], np.float32) for j in range(4)], 0) for s in range(2)])
    return yp, ys


def kernel(**inputs):
    cfg = Cfg(16, 32, 4)
    yp, ys = run(cfg, inputs)
    return (yp, ys)
```
